# Optimizing a Trainium2 kernel written in Bass

```python
import jax, jax.numpy as jnp
from jax import lax
import numpy as np

D_MODEL = 1024
BATCH = 2
SEQ = 8192
DEPTH = 1

ATTN_WIDTH = D_MODEL // 2
POOL_WIDTH = D_MODEL - ATTN_WIDTH
HEAD_DIM = 64
N_Q_HEADS = ATTN_WIDTH // HEAD_DIM
N_KV_HEADS = 2
GQA_GROUP = N_Q_HEADS // N_KV_HEADS
KV_WIDTH = N_KV_HEADS * HEAD_DIM
WINDOW = 128
BLOCK = 128
ROPE_THETA = 10000.0
POOL_SIZES = (2, 4, 8, 16)
N_POOL_GROUPS = len(POOL_SIZES)
POOL_GROUP_WIDTH = POOL_WIDTH // N_POOL_GROUPS
IN_WIDTH = ATTN_WIDTH + 2 * KV_WIDTH + POOL_WIDTH
D_FF = -(-8 * D_MODEL // (3 * 256)) * 256
RMS_EPS = 1e-5

kernel_name = "hybrid_swa_sink_multiscale_pool_block"


def rmsnorm(x, g):
    xf = x.astype(jnp.float32)
    y = xf * lax.rsqrt(jnp.mean(xf * xf, axis=-1, keepdims=True) + RMS_EPS)
    return (y * g.astype(jnp.float32)).astype(x.dtype)


def rope_tables(seq):
    inv_freq = 1.0 / (ROPE_THETA ** (jnp.arange(0, HEAD_DIM, 2, dtype=jnp.float32) / HEAD_DIM))
    ang = jnp.arange(seq, dtype=jnp.float32)[:, None] * inv_freq[None, :]
    return jnp.cos(ang), jnp.sin(ang)


def apply_rope(t, cos, sin):
    t1, t2 = jnp.split(t.astype(jnp.float32), 2, axis=-1)
    c = cos[None, :, None, :]
    s = sin[None, :, None, :]
    return jnp.concatenate([t1 * c - t2 * s, t2 * c + t1 * s], axis=-1).astype(t.dtype)


def sliding_window_attention_with_sinks(q, k, v, sinks):
    b, s = q.shape[0], q.shape[1]
    nb = s // BLOCK
    qb = q.reshape(b, nb, BLOCK, N_KV_HEADS, GQA_GROUP, HEAD_DIM)

    def band(t):
        t = t.reshape(b, nb, BLOCK, N_KV_HEADS, HEAD_DIM)
        prev = jnp.pad(t, ((0, 0), (1, 0), (0, 0), (0, 0), (0, 0)))[:, :-1]
        return jnp.concatenate([prev, t], axis=2)

    kb, vb = band(k), band(v)
    scores = jnp.einsum('bnqkgd,bnskd->bnkgqs', qb, kb,
                        preferred_element_type=jnp.float32) * (HEAD_DIM ** -0.5)
    qi = jnp.arange(BLOCK)[:, None] + BLOCK
    sj = jnp.arange(2 * BLOCK)[None, :]
    delta = qi - sj
    in_window = (delta >= 0) & (delta < WINDOW)
    key_pos = jnp.arange(nb)[:, None] * BLOCK + sj - BLOCK
    mask = in_window[None] & (key_pos >= 0)[:, None, :]
    scores = jnp.where(mask[None, :, None, None], scores, -jnp.inf)
    sink = sinks.astype(jnp.float32).reshape(N_KV_HEADS, GQA_GROUP)[None, None, :, :, None, None]
    m = jnp.maximum(jnp.max(scores, axis=-1, keepdims=True), sink)
    p = jnp.exp(scores - m)
    p = p / (jnp.sum(p, axis=-1, keepdims=True) + jnp.exp(sink - m))
    out = jnp.einsum('bnkgqs,bnskd->bnqkgd', p.astype(v.dtype), vb)
    return out.reshape(b, s, N_Q_HEADS * HEAD_DIM)


def multiscale_causal_pool(u, w_pool, b_pool, pool_scale):
    b, s, _ = u.shape
    uf = u.astype(jnp.float32).reshape(b, s, N_POOL_GROUPS, POOL_GROUP_WIDTH)
    cs = jnp.pad(jnp.cumsum(uf, axis=1), ((0, 0), (1, 0), (0, 0), (0, 0)))
    t = jnp.arange(s)[:, None]
    sizes = jnp.array(POOL_SIZES, dtype=jnp.int32)[None, :]
    start = jnp.maximum(t + 1 - sizes, 0)
    g_idx = jnp.arange(N_POOL_GROUPS)[None, :]
    window_sum = cs[:, 1:] - cs[:, start, g_idx]
    count = (t + 1 - start).astype(jnp.float32)
    mixed = window_sum / count[None, :, :, None] - uf
    y = jnp.einsum('bsgc,gcd->bsgd', mixed.astype(u.dtype), w_pool) + b_pool
    y = y * pool_scale
    return y.reshape(b, s, POOL_WIDTH)


def setup_inputs(seed: int = 0) -> dict:
    key = jax.random.key(seed)
    ks = jax.random.split(key, 16)
    f32 = jnp.float32
    nrm = lambda k, shape, scale: jax.random.normal(k, shape, f32) * scale
    return {
        "x": nrm(ks[0], (BATCH, SEQ, D_MODEL), 1.0),
        "g_mix": 1.0 + nrm(ks[1], (DEPTH, D_MODEL), 0.02),
        "w_in": nrm(ks[2], (DEPTH, D_MODEL, IN_WIDTH), D_MODEL ** -0.5),
        "b_in": nrm(ks[3], (DEPTH, IN_WIDTH), 0.02),
        "sinks": nrm(ks[4], (DEPTH, N_Q_HEADS), 1.0),
        "w_pool": nrm(ks[5], (DEPTH, N_POOL_GROUPS, POOL_GROUP_WIDTH, POOL_GROUP_WIDTH), POOL_GROUP_WIDTH ** -0.5),
        "b_pool": nrm(ks[6], (DEPTH, N_POOL_GROUPS, POOL_GROUP_WIDTH), 0.02),
        "pool_scale": 1.0 + nrm(ks[7], (DEPTH, N_POOL_GROUPS, POOL_GROUP_WIDTH), 0.1),
        "w_out": nrm(ks[8], (DEPTH, ATTN_WIDTH + POOL_WIDTH, D_MODEL), (ATTN_WIDTH + POOL_WIDTH) ** -0.5),
        "b_out": nrm(ks[9], (DEPTH, D_MODEL), 0.02),
        "g_ffn": 1.0 + nrm(ks[10], (DEPTH, D_MODEL), 0.02),
        "w_gate": nrm(ks[11], (DEPTH, D_MODEL, D_FF), D_MODEL ** -0.5),
        "w_up": nrm(ks[12], (DEPTH, D_MODEL, D_FF), D_MODEL ** -0.5),
        "w_down": nrm(ks[13], (DEPTH, D_FF, D_MODEL), D_FF ** -0.5),
        "g_final": 1.0 + nrm(ks[14], (D_MODEL,), 0.02),
    }


def reference(x, g_mix, w_in, b_in, sinks, w_pool, b_pool, pool_scale, w_out, b_out,
              g_ffn, w_gate, w_up, w_down, g_final):
    b, s, _ = x.shape
    cos, sin = rope_tables(s)
    for i in range(DEPTH):
        h = rmsnorm(x, g_mix[i])
        z = h @ w_in[i] + b_in[i]
        q = z[..., :ATTN_WIDTH].reshape(b, s, N_Q_HEADS, HEAD_DIM)
        k = z[..., ATTN_WIDTH:ATTN_WIDTH + KV_WIDTH].reshape(b, s, N_KV_HEADS, HEAD_DIM)
        v = z[..., ATTN_WIDTH + KV_WIDTH:ATTN_WIDTH + 2 * KV_WIDTH].reshape(b, s, N_KV_HEADS, HEAD_DIM)
        u = z[..., ATTN_WIDTH + 2 * KV_WIDTH:]
        q = apply_rope(q, cos, sin)
        k = apply_rope(k, cos, sin)
        attn = sliding_window_attention_with_sinks(q, k, v, sinks[i])
        pool = multiscale_causal_pool(u, w_pool[i], b_pool[i], pool_scale[i])
        x = x + jnp.concatenate([attn, pool], axis=-1) @ w_out[i] + b_out[i]
        h = rmsnorm(x, g_ffn[i])
        x = x + (jax.nn.silu(h @ w_gate[i]) * (h @ w_up[i])) @ w_down[i]
    return rmsnorm(x, g_final)
```

```python
import numpy as np
import concourse.bass as bass
import concourse.mybir as mybir
from concourse.bass_utils import run_bass_kernel_spmd

F32 = mybir.dt.float32
BF16 = mybir.dt.bfloat16
ALU = mybir.AluOpType
AF = mybir.ActivationFunctionType
AX = mybir.AxisListType

D = 1024
NB = 16
TOK = 2048
DFF = 2816
NPASS = 11
EPS = 1e-5
GB = 2
RING = 2 * GB + 1

DEBUG_X1 = False
USE_LN_RECIP = True
USE_SCHED = True
P0_BANKS = True
TAIL_LAG = 8
CHUNK_ORDER = [0, 1, 2, 3, 4]
SCHED_LIMIT = [None]
NO_INTERLEAVE = False


def I(method, *args, **kw):
    return (method, args, kw)


class Buf:
    __slots__ = ("name", "w", "r")

    def __init__(self, name):
        self.name = name
        self.w = None
        self.r = []


class Op:
    __slots__ = ("eng", "fn", "deps", "idx", "signal", "val", "dma", "sem", "seq")

    def __init__(self, eng, fn, deps, dma):
        self.eng = eng
        self.fn = fn
        self.deps = deps
        self.dma = dma
        self.signal = False
        self.val = None
        self.sem = None


class Prog:
    ENGS = ("pe", "act", "dve", "pool", "sp")

    def __init__(self):
        self.ops = {e: [] for e in self.ENGS}
        self.nseq = 0

    def op(self, eng, fn, reads=(), writes=(), dma=False, extra=()):
        deps = set(extra)
        for b in reads:
            if b.w is not None:
                deps.add(b.w)
        for b in writes:
            if b.w is not None:
                deps.add(b.w)
            deps.update(b.r)
        o = Op(eng, fn, deps, dma)
        o.seq = self.nseq
        self.nseq += 1
        for b in reads:
            b.r.append(o)
        for b in writes:
            b.w = o
            b.r = []
        o.idx = len(self.ops[eng])
        self.ops[eng].append(o)
        return o


    @staticmethod
    def _free(ap):
        n = 1
        for d in ap.shape[1:]:
            n *= d
        return n

    def _est(self, o):
        m, args, kw = o.fn
        out = kw.get("out", args[0] if args else None)
        n = self._free(out) if out is not None else 1
        if o.dma:
            nbytes = n * out.shape[0] * 4
            return (1000.0 if o.eng == "pool" else 120.0), 2000.0 + nbytes / 320.0
        if o.eng == "pe":
            if m == "transpose":
                n = 128
            else:
                n = self._free(kw["rhs"])
            return max(64, n) / 2.2 + 12.0, None
        if o.eng == "act":
            return 200.0 + n / 1.2 + (90.0 if kw.get("accum_out") is not None else 0.0), None
        if o.eng == "dve":
            return 190.0 + n * 1.05, None
        if o.eng == "pool":
            if m == "memset":
                return 150.0, None
            return 250.0 + n * 2.1, None
        return 100.0, None

    def schedule(self, limit=None):
        allops = [o for e in self.ENGS for o in self.ops[e]]
        rest = {e: [] for e in self.ENGS}
        if limit is not None:
            for e in self.ENGS:
                rest[e] = [o for o in self.ops[e] if o.seq >= limit]
            allops = [o for o in allops if o.seq < limit]
        order = {}
        allops.sort(key=lambda o: o.seq)
        succ = {id(o): [] for o in allops}
        indeg = {}
        for o in allops:
            indeg[id(o)] = len(o.deps)
            for d in o.deps:
                succ[id(d)].append(o)
        cp = {}
        for o in reversed(allops):
            b_, l_ = self._est(o)
            d_ = min(l_, 4000.0) if l_ is not None else b_
            cp[id(o)] = d_ + max([cp[id(s_)] + 280.0 for s_ in succ[id(o)]], default=0.0)
        fin = {}
        self._bus = 0.0
        ready = {e: [] for e in self.ENGS}
        efree = {e: 0.0 for e in self.ENGS}
        for o in allops:
            if indeg[id(o)] == 0:
                ready[o.eng].append((0.0, o.seq, o))
        new = {e: [] for e in self.ENGS}
        n_left = len(allops)
        while n_left:
            best = None
            for e in self.ENGS:
                if not ready[e]:
                    continue
                T = efree[e]
                c = min(ready[e], key=lambda r: (max(r[0], T), -cp[id(r[2])], r[1]))
                st = max(c[0], T)
                if best is None or (st, c[1]) < (best[0], best[1][1]):
                    best = (st, c, e)
            st, c, e = best
            ready[e].remove(c)
            o = c[2]
            busy, lat = self._est(o)
            efree[e] = st + busy
            if o.dma:
                xfer = lat - 2000.0
                bus = max(self._bus, st + busy) + xfer
                self._bus = bus
                fin[id(o)] = bus + 3000.0
            else:
                fin[id(o)] = st + (lat if lat is not None else busy)
            new[e].append(o)
            n_left -= 1
            for s_ in succ[id(o)]:
                indeg[id(s_)] -= 1
                if indeg[id(s_)] == 0:
                    rt = 0.0
                    for d in s_.deps:
                        hop = 60.0 if d.eng == s_.eng else 280.0
                        rt = max(rt, fin[id(d)] + hop)
                    ready[s_.eng].append((rt, s_.seq, s_))
        for e in self.ENGS:
            self.ops[e] = new[e] + rest[e]
        self.sim_end = max(fin.values())

    def emit(self, nc, block, sems, dma_sems, final_wait_ops):
        pos = {}
        for e in self.ENGS:
            for i, o in enumerate(self.ops[e]):
                pos[id(o)] = i
        for e in self.ENGS:
            for o in self.ops[e]:
                last = {}
                for d in o.deps:
                    if d.dma:
                        d.signal = True
                    elif d.eng == "pe" and o.eng == "pe" and not o.dma:
                        continue
                    elif d.eng not in last or pos[id(d)] > pos[id(last[d.eng])]:
                        last[d.eng] = d
                for d in last.values():
                    d.signal = True
        for o in final_wait_ops:
            o.signal = True
        dma_prev = {}
        for e in self.ENGS:
            cnt = 0
            ndma = 0
            for o in self.ops[e]:
                if o.dma:
                    pool = dma_sems[e]
                    k = ndma % len(pool)
                    u = ndma // len(pool)
                    o.sem = pool[k]
                    o.val = 16 * (u + 1)
                    ndma += 1
                elif o.signal:
                    cnt += 1
                    o.sem = sems[e]
                    o.val = cnt

        def run(e, eng):
            waited = {}
            for o in self.ops[e]:
                need = {}
                for d in o.deps:
                    if (not d.dma) and d.eng == "pe" and e == "pe" and not o.dma:
                        continue
                    if d.sem is None:
                        continue
                    k = d.sem
                    if d.val > need.get(k.num, (None, 0))[1]:
                        need[k.num] = (k, d.val)
                if o.dma and o.val > 16:
                    k = o.sem
                    if o.val - 16 > need.get(k.num, (None, 0))[1]:
                        need[k.num] = (k, o.val - 16)
                for num, (k, v) in need.items():
                    if waited.get(num, 0) >= v:
                        continue
                    eng.wait_ge(k, v)
                    waited[num] = v
                ins = getattr(eng, o.fn[0])(*o.fn[1], **o.fn[2])
                if o.dma:
                    ins.then_inc(o.sem, 16)
                elif o.signal:
                    ins.then_inc(o.sem, 1)
            if e == "sp":
                for o in final_wait_ops:
                    eng.wait_ge(o.sem, o.val)

        @block.tensor
        def _(eng):
            run("pe", eng)

        @block.scalar
        def _(eng):
            run("act", eng)

        @block.vector
        def _(eng):
            run("dve", eng)

        @block.gpsimd
        def _(eng):
            run("pool", eng)

        @block.sync
        def _(eng):
            run("sp", eng)


class Sbuf:
    def __init__(self, nc, base, limit):
        self.nc = nc
        self.off = base
        self.limit = limit
        self.n = 0

    def alloc(self, name, shape, dtype, at=None):
        esz = 4 if dtype == F32 else 2
        nbytes = esz
        for s in shape[1:]:
            nbytes *= s
        if at is None:
            at = self.off
            self.off = (self.off + nbytes + 31) // 32 * 32
            assert self.off <= self.limit, (name, self.off, self.limit)
        self.n += 1
        t = self.nc.alloc_sbuf_tensor_at(f"{name}_{self.n}", list(shape), dtype, offset=at)
        return t, at


def build_program():
    nc = bass.Bass("TRN2", target_bir_lowering=False)
    P = Prog()

    def dram_in(name, shape):
        return nc.dram_tensor(name, list(shape), F32, kind="ExternalInput").ap()

    xh = dram_in("xh", [17 * 128, D])
    wqk_d = dram_in("wqk", [D, 640])
    wvu_d = dram_in("wvu", [D, 640])
    wout_d = dram_in("wout", [D, D])
    wpool_d = dram_in("wpool", [4, 128, 128])
    wgate_d = dram_in("wgate", [D, DFF])
    wup_d = dram_in("wup", [D, DFF])
    wdown_d = dram_in("wdown", [DFF, D])
    gmix_d = dram_in("gmix", [128, 8])
    gffn_d = dram_in("gffn", [128, 8])
    bqk_d = dram_in("bqk", [128, 5])
    bqksw_d = dram_in("bqksw", [128, 5])
    bvu_d = dram_in("bvu", [1, 640])
    bout_d = dram_in("bout", [1, D])
    gfin_d = dram_in("gfin", [1, D])
    bpool_d = dram_in("bpool", [128, 4])
    pscale_d = dram_in("pscale", [128, 4])
    sinks_d = dram_in("sinks", [1, 8])
    flag_d = dram_in("flag", [128, 1])
    ident_d = dram_in("ident", [128, 128])
    apool_d = dram_in("apool", [16, 128, 128])
    cos_d = dram_in("cosT", [128, 17 * 128])
    sin_d = dram_in("sinT", [128, 17 * 128])
    out_d = nc.dram_tensor("out", [TOK, D], F32, kind="ExternalOutput").ap()

    total = nc.sbuf_bytes_remaining
    asize = (total - 2048) // 64 * 64
    arena = nc.alloc_sbuf_tensor("arena", [128, asize // 4], F32)
    base = nc.lookup_mloc(arena).addr
    assert base % 32 == 0
    sb = Sbuf(nc, base, base + asize)

    x1, _ = sb.alloc("x1", [128, NB, D], F32)
    h2T, h2T_off = sb.alloc("h2T", [128, 8, TOK], BF16)
    xhalo, _ = sb.alloc("xhalo", [128, D], F32, at=h2T_off)
    c_gmix, _ = sb.alloc("c_gmix", [128, 8], F32)
    c_gffn, _ = sb.alloc("c_gffn", [128, 8], F32)
    c_bqk, _ = sb.alloc("c_bqk", [128, 5], F32)
    c_bqksw, _ = sb.alloc("c_bqksw", [128, 5], F32)
    c_bpool, _ = sb.alloc("c_bpool", [128, 4], F32)
    c_pscale, _ = sb.alloc("c_pscale", [128, 4], F32)
    c_esink, _ = sb.alloc("c_esink", [128, 8], F32)
    c_flag, _ = sb.alloc("c_flag", [128, 1], F32)
    c_nhalf, _ = sb.alloc("c_nhalf", [128, 1], F32)
    c_ident, _ = sb.alloc("c_ident", [128, 128], BF16)
    c_bvu, _ = sb.alloc("c_bvu", [128, 640], F32)
    c_bout, _ = sb.alloc("c_bout", [128, D], F32)
    c_gfin, gfin_off = sb.alloc("c_gfin", [128, D], F32)
    c_eps, _ = sb.alloc("c_eps", [128, 1], F32)
    c_apool, _ = sb.alloc("c_apool", [128, 16, 128], BF16)
    ss, _ = sb.alloc("ss", [128, 64], F32)
    rstd, _ = sb.alloc("rstd", [128, 64], F32)
    ffn_bufs = []
    g0, _ = sb.alloc("gate0", [128, 8, 256], BF16)
    u0, _ = sb.alloc("up0", [128, 8, 256], BF16)
    d0, _ = sb.alloc("down0", [128, 2, D], BF16)
    ffn_bufs.append((g0, u0, d0))
    w_region = sb.off
    wqk, _ = sb.alloc("wqk", [128, 8, 640], BF16)
    wvu, _ = sb.alloc("wvu", [128, 8, 640], BF16)
    wout, _ = sb.alloc("wout", [128, 8, D], BF16)
    wpool, _ = sb.alloc("wpool", [128, 4, 128], BF16)
    o = w_region
    for i in (1, 2):
        g_, _ = sb.alloc(f"gate{i}", [128, 8, 256], BF16, at=o); o += 8 * 256 * 2
        u_, _ = sb.alloc(f"up{i}", [128, 8, 256], BF16, at=o); o += 8 * 256 * 2
        d_, _ = sb.alloc(f"down{i}", [128, 2, D], BF16, at=o); o += 2 * D * 2
        ffn_bufs.append((g_, u_, d_))
    assert o <= sb.off
    scratch = sb.off
    GT = GB * 128
    xs, _ = sb.alloc("xs", [128, D], BF16)
    junk, _ = sb.alloc("junk", [128, D], BF16, at=gfin_off)
    hT, _ = sb.alloc("hT", [128, 8, GT], BF16)
    xs2, _ = sb.alloc("xs2", [128, D], BF16)
    qT = [sb.alloc(f"qT{i}", [128, 4, GT], BF16)[0] for i in range(2)]
    K0, _ = sb.alloc("K0", [128, RING, 128], BF16)
    K1, _ = sb.alloc("K1", [128, RING, 128], BF16)
    vaug, _ = sb.alloc("vaug", [128, RING, 2, 128], BF16)
    utok, _ = sb.alloc("utok", [128, RING, 512], BF16)
    cosb = [sb.alloc(f"cos{i}", [128, GT], F32)[0] for i in range(1)]
    sinb = [sb.alloc(f"sin{i}", [128, GT], F32)[0] for i in range(1)]
    ropeA, _ = sb.alloc("ropeA", [128, GT], F32)
    ropeB, _ = sb.alloc("ropeB", [128, GT], F32)
    mixedT, _ = sb.alloc("mixedT", [128, 4, GT], BF16)
    poolT, _ = sb.alloc("poolT", [128, 4, GT], BF16)
    attnT, _ = sb.alloc("attnT", [128, 4, GT], BF16)
    Pt = [sb.alloc(f"P{i}", [128, 4, 128], BF16)[0] for i in range(4)]
    rden = [sb.alloc(f"rden{i}", [128, 4, 128], F32)[0] for i in range(2)]
    mk, _ = sb.alloc("mk", [128, 2, 128], BF16)
    es_bf, _ = sb.alloc("es_bf", [128, 8], BF16)
    sel, _ = sb.alloc("sel", [128, 128], BF16)
    hT_alt, _ = sb.alloc("hT_alt", [128, 8, GT], BF16)
    phase1_end = sb.off
    sb2 = Sbuf(nc, scratch, sb.limit)
    sb2.n = 1000
    xs_2, _ = sb2.alloc("xs_2", [128, D], BF16)
    junk_2, _ = sb2.alloc("junk_2", [128, D], BF16)
    junk2 = junk_2
    sg = [sb2.alloc(f"sg{i}", [128, 512], F32)[0] for i in range(2)]
    actT = [sb2.alloc(f"actT{i}", [128, 512], BF16)[0] for i in range(4)]
    tmpf = [sb2.alloc(f"tmpf{i}", [128, D], F32)[0] for i in range(2)]
    print("SBUF plan: phase1 end", phase1_end, "phase2 end", sb2.off, "limit", sb.limit)

    banks = []
    for i in range(8):
        banks.append(nc.alloc_psum_tensor(f"bank{i}", [128, 512], F32))
    bank_bufs = [Buf(f"bank{i}") for i in range(8)]

    sems = {}
    dma_sems = {}

    def dma(q, out, in_, reads=(), writes=(), extra=()):
        return P.op(q, I("dma_start", out=out, in_=in_), reads, writes, dma=True, extra=extra)

    B = {}

    def buf(name):
        if name not in B:
            B[name] = Buf(name)
        return B[name]

    xbuf = [buf(f"x1_{i}") for i in range(NB)]
    xhalo_b = buf("xhalo")

    def xslot(b):
        return (xhalo[:, :], xhalo_b) if b == 0 else (x1[:, b - 1, :], xbuf[b - 1])

    def load_x(b, extra=()):
        ap, bb = xslot(b)
        dma("sp", ap, xh[b * 128:(b + 1) * 128, :], writes=[bb], extra=extra)

    P.op("pool", I("memset", c_eps[:], EPS), writes=[buf("c_eps")])
    P.op("pool", I("memset", c_nhalf[:], -0.5), writes=[buf("c_nhalf")])
    load_x(0)
    load_x(1)
    load_x(2)
    dma("pool", wqk[:], wqk_d.rearrange("(k p) n -> p k n", p=128), writes=[buf("wqk")])
    dma("pool", c_ident[:], ident_d[:, :], writes=[buf("ident")])
    dma("sp", c_gmix[:], gmix_d[:, :], writes=[buf("c_gmix")])
    dma("sp", c_bqk[:], bqk_d[:, :], writes=[buf("c_bqk")])
    dma("sp", c_bqksw[:], bqksw_d[:, :], writes=[buf("c_bqksw")])
    dma("sp", c_bvu[:], bvu_d.partition_broadcast(128).squeeze(1), writes=[buf("c_bvu")])
    dma("sp", c_flag[:], flag_d[:, :], writes=[buf("c_flag")])
    dma("pool", wvu[:], wvu_d.rearrange("(k p) n -> p k n", p=128), writes=[buf("wvu")])
    dma("pool", c_apool[:], apool_d.rearrange("m p t -> p m t"), writes=[buf("c_apool")])
    dma("pool", wpool[:], wpool_d.rearrange("g c d -> c g d"), writes=[buf("wpool")])
    dma("sp", c_esink[:], sinks_d.partition_broadcast(128).squeeze(1), writes=[buf("c_esink")])
    dma("sp", c_bpool[:], bpool_d[:, :], writes=[buf("c_bpool")])
    dma("sp", c_pscale[:], pscale_d[:, :], writes=[buf("c_pscale")])
    dma("sp", c_bout[:], bout_d.partition_broadcast(128).squeeze(1), writes=[buf("c_bout")])
    dma("sp", c_gffn[:], gffn_d[:, :], writes=[buf("c_gffn")])
    dma("pool", wout[:], wout_d.rearrange("(k p) n -> p k n", p=128), writes=[buf("wout")])
    for b in range(3, 5):
        load_x(b)

    ffn_b = [(buf(f"ffg{i}"), buf(f"ffu{i}"), buf(f"ffd{i}")) for i in range(3)]

    def load_ffn(j):
        r = j % 3
        g_, u_, d_ = ffn_bufs[r]
        gb_, ub_, db_ = ffn_b[r]
        extra_w = [buf("wqk"), buf("wvu"), buf("wout"), buf("wpool")] if r in (1, 2) and j < 3 else []
        dma("pool", g_[:], wgate_d[:, j * 256:(j + 1) * 256].rearrange("(k p) n -> p k n", p=128),
            writes=[gb_] + extra_w)
        dma("pool", u_[:], wup_d[:, j * 256:(j + 1) * 256].rearrange("(k p) n -> p k n", p=128),
            writes=[ub_] + extra_w)
        dma("pool", d_[:], wdown_d[j * 256:(j + 1) * 256, :].rearrange("(k p) n -> p k n", p=128),
            writes=[db_] + extra_w)

    P.op("pool", I("memset", K0[:], 0.0), writes=[buf("K0z")])
    P.op("pool", I("memset", K1[:], 0.0), writes=[buf("K1z")])

    ss_col = [0]
    BK_T0, BK_QV, BK_U, BK_M, BK_S0, BK_S1, BK_O0, BK_O1 = 0, 1, 2, 3, 4, 5, 6, 7
    NEG = -30000.0
    bankQ = bankV = bank_bufs[1]

    P.op("pool", I("memset", mk[:], 0.0), writes=[buf("maskb0"), buf("maskb1")])
    P.op("pool", I("affine_select", out=mk[:, 0, :], in_=mk[:, 0, :], pattern=[[-1, 128]],
                   compare_op=ALU.is_ge, fill=NEG, base=-1, channel_multiplier=1),
         reads=[buf("maskb0")], writes=[buf("maskb0")])
    P.op("pool", I("affine_select", out=mk[:, 1, :], in_=mk[:, 1, :], pattern=[[1, 128]],
                   compare_op=ALU.is_ge, fill=NEG, base=0, channel_multiplier=-1),
         reads=[buf("maskb1")], writes=[buf("maskb1")])
    P.op("pool", I("memset", sel[:, :], 0.0), writes=[buf("est_s0")])
    P.op("pool", I("memset", sel[0:1, 64:128], 1.0), reads=[buf("est_s0")], writes=[buf("est_s1")])
    est_bufs = [buf("est"), buf("est_s0"), buf("est_s1")]

    def stats(cols, x_list):
        c0 = cols[0]
        n = len(cols)
        sbufs = []
        for c, (x_ap, x_buf) in zip(cols, x_list):
            sb_ = buf(f"ss{c}")
            sbufs.append(sb_)
            P.op("act", I("activation", out=junk[:], in_=x_ap, func=AF.Square, accum_out=ss[:, c:c + 1]),
                 reads=[x_buf], writes=[buf("junk"), sb_])
        rbs = [buf(f"rstd{c}") for c in cols]
        P.op("act", I("activation", out=rstd[:, c0:c0 + n], in_=ss[:, c0:c0 + n], func=AF.Ln,
                      scale=1.0 / D, bias=c_eps[:, 0:1]),
             reads=sbufs + [buf("c_eps")], writes=rbs)
        P.op("act", I("activation", out=rstd[:, c0:c0 + n], in_=rstd[:, c0:c0 + n], func=AF.Exp, scale=-0.5),
             reads=rbs, writes=rbs)

    def norm_T(x_ap, x_buf, c, gcol, dstT, dst_bufs, col0, tagT, xs_t, xs_name, tbank):
        rb_ = buf(f"rstd{c}")
        P.op("act", I("activation", out=xs_t[:], in_=x_ap, func=AF.Copy, scale=rstd[:, c:c + 1]),
             reads=[x_buf, rb_], writes=[buf(xs_name)])
        yield
        yield
        tb = banks[tbank]
        tbf = tb[:, :].bitcast(BF16)
        for kc in range(8):
            P.op("pe", I("transpose", out=tbf[:, kc * 128:(kc + 1) * 128],
                         in_=xs_t[:, kc * 128:(kc + 1) * 128], identity=c_ident[:]),
                 reads=[buf(xs_name), buf("ident")], writes=[bank_bufs[tbank]])
        P.op("dve", I("tensor_tensor",
                      out=dstT[:, 0:8, col0:col0 + 128],
                      in0=tbf.rearrange("p (k t) -> p k t", k=8),
                      in1=gcol[:, 0:8].unsqueeze(2).to_broadcast([128, 8, 128]), op=ALU.mult),
             reads=[bank_bufs[tbank], buf(tagT)], writes=dst_bufs)
        yield

    hT_b = buf("hT")
    hT_list = [hT, hT_alt]
    hT_bufs = [hT_b, buf("hT_alt")]
    qT_bs = [buf("qT0"), buf("qT1")]
    Kb = [buf(f"K_{i}") for i in range(RING)]
    Vb = [buf(f"V_{i}") for i in range(RING)]
    Ub = [buf(f"U_{i}") for i in range(RING)]
    tabb = [buf("tab0"), buf("tab1")]
    h2T_b = [buf(f"h2T_{i}") for i in range(NB)]

    groups = [[0]] + [list(range(1 + GB * i, 1 + GB * (i + 1))) for i in range(NB // GB)]

    ss_col[0] = 17

    def stage1(gi):
        hT = hT_list[gi % 2]
        hT_b = hT_bufs[gi % 2]
        blocks = groups[gi]
        nb = len(blocks)
        ntok = nb * 128
        b0 = blocks[0]
        halo = (b0 == 0)
        qTg = qT[gi % 2]
        qT_b = qT_bs[gi % 2]
        tb_i = 0
        t_op = dma("sp", cosb[tb_i][:, 0:ntok], cos_d[:, b0 * 128:b0 * 128 + ntok], writes=[tabb[tb_i]])
        if gi == 1:
            for b_ in range(5, 17):
                load_x(b_, extra=[t_op])
            stats(list(range(5, 11)), [xslot(b_) for b_ in range(5, 11)])
            stats(list(range(11, 17)), [xslot(b_) for b_ in range(11, 17)])
        if gi == 2:
            load_ffn(0)
        dma("sp", sinb[tb_i][:, 0:ntok], sin_d[:, b0 * 128:b0 * 128 + ntok], writes=[tabb[tb_i]])
        for li, b in enumerate(blocks):
            xap, xb_ = xslot(b)
            yield from norm_T(xap, xb_, b, c_gmix, hT, [hT_b], li * 128, "c_gmix", xs, "xs", BK_T0)
            if not halo:
                P.op("pool", I("tensor_tensor", out=xap, in0=xap, in1=c_bout[:], op=ALU.add),
                     reads=[xb_, buf("c_bout")], writes=[xb_])
        for li, b in enumerate(blocks):
            s = b % RING
            for kc in range(8):
                P.op("pe", I("matmul", out=banks[BK_U][:, :], lhsT=hT[:, kc, li * 128:(li + 1) * 128],
                             rhs=wvu[:, kc, 0:512], start=(kc == 0), stop=(kc == 7)),
                     reads=[buf("wvu"), hT_b], writes=[bank_bufs[BK_U]])
            for kc in range(8):
                P.op("pe", I("matmul", out=banks[BK_QV][:, 256:384], lhsT=hT[:, kc, li * 128:(li + 1) * 128],
                             rhs=wvu[:, kc, 512:640], start=(kc == 0), stop=(kc == 7)),
                     reads=[buf("wvu"), hT_b], writes=[bankV])
            yield
            yield
            P.op("dve", I("tensor_tensor", out=utok[:, s, :], in0=banks[BK_U][:, :], in1=c_bvu[:, 0:512],
                          op=ALU.add),
                 reads=[bank_bufs[BK_U], buf("c_bvu")], writes=[Ub[s]])
            P.op("dve", I("tensor_tensor",
                          out=vaug[:, s, :, 0:64],
                          in0=banks[BK_QV][:, 256:384].rearrange("p (k d) -> p k d", k=2),
                          in1=c_bvu[:, 512:640].rearrange("p (k d) -> p k d", k=2), op=ALU.add),
                 reads=[bankV, buf("c_bvu")], writes=[Vb[s]])
            P.op("pool", I("memset", vaug[:, s, :, 64:128], 1.0), reads=[], writes=[buf(f"Vones_{s}")])
            if b == 0:
                P.op("dve", I("tensor_scalar", out=vaug[:, s, :, :], in0=vaug[:, s, :, :],
                              scalar1=c_flag[:, 0:1], scalar2=None, op0=ALU.mult),
                     reads=[Vb[s], buf(f"Vones_{s}"), buf("c_flag")], writes=[Vb[s], buf(f"Vones_{s}")])
            yield
        chunks = [4] if halo else CHUNK_ORDER
        for ci, c in enumerate(chunks):
            for kc in range(8):
                P.op("pe", I("matmul", out=banks[BK_QV][:, 0:ntok], lhsT=wqk[:, kc, c * 128:(c + 1) * 128],
                             rhs=hT[:, kc, 0:ntok], start=(kc == 0), stop=(kc == 7)),
                     reads=[buf("wqk"), hT_b], writes=[bankQ])
            yield
            yield
            Z = banks[BK_QV]
            P.op("dve", I("scalar_tensor_tensor",
                          out=ropeA[:, 0:ntok], in0=Z[:, 0:ntok], scalar=c_bqk[:, c:c + 1],
                          in1=cosb[tb_i][:, 0:ntok], op0=ALU.add, op1=ALU.mult),
                 reads=[bankQ, buf("c_bqk"), tabb[tb_i]], writes=[buf("ropeA")])
            P.op("dve", I("scalar_tensor_tensor",
                          out=ropeB[0:64, 0:ntok], in0=Z[64:128, 0:ntok], scalar=c_bqksw[0:64, c:c + 1],
                          in1=sinb[tb_i][0:64, 0:ntok], op0=ALU.add, op1=ALU.mult),
                 reads=[bankQ, buf("c_bqksw"), tabb[tb_i]], writes=[buf("ropeB0")])
            P.op("dve", I("scalar_tensor_tensor",
                          out=ropeB[64:128, 0:ntok], in0=Z[0:64, 0:ntok], scalar=c_bqksw[64:128, c:c + 1],
                          in1=sinb[tb_i][64:128, 0:ntok], op0=ALU.add, op1=ALU.mult),
                 reads=[bankQ, buf("c_bqksw"), tabb[tb_i]], writes=[buf("ropeB1")])
            rb = [buf("ropeA"), buf("ropeB0"), buf("ropeB1")]
            if c < 4:
                P.op("pool", I("tensor_tensor", out=qTg[:, c, 0:ntok], in0=ropeA[:, 0:ntok],
                               in1=ropeB[:, 0:ntok], op=ALU.add),
                     reads=rb, writes=[qT_b])
            else:
                for li, b in enumerate(blocks):
                    s = b % RING
                    for (q0, Kt) in ((0, K0), (32, K1), (64, K0), (96, K1)):
                        P.op("pool", I("tensor_tensor",
                                       out=Kt[q0:q0 + 32, s, :], in0=ropeA[q0:q0 + 32, li * 128:(li + 1) * 128],
                                       in1=ropeB[q0:q0 + 32, li * 128:(li + 1) * 128], op=ALU.add),
                             reads=rb + [buf("K0z"), buf("K1z")], writes=[Kb[s]])
            yield

    def stage23(gi):
        blocks = groups[gi]
        nb = len(blocks)
        ntok = nb * 128
        tb_i = gi % 2
        qTg = qT[tb_i]
        qT_b = qT_bs[tb_i]
        for li, b in enumerate(blocks):
            s, sp_ = b % RING, (b - 1) % RING
            first = 8 if b == 1 else 0
            Mb = banks[BK_M]
            for g in range(4):
                P.op("pe", I("matmul", out=Mb[:, g * 128:(g + 1) * 128], lhsT=utok[:, s, g * 128:(g + 1) * 128],
                             rhs=c_apool[:, first + g, :], start=True, stop=False),
                     reads=[Ub[s], buf("c_apool")], writes=[bank_bufs[BK_M]])
                P.op("pe", I("matmul", out=Mb[:, g * 128:(g + 1) * 128], lhsT=utok[:, sp_, g * 128:(g + 1) * 128],
                             rhs=c_apool[:, first + 4 + g, :], start=False, stop=True),
                     reads=[Ub[sp_], buf("c_apool")], writes=[bank_bufs[BK_M]])
            yield
            P.op("act", I("activation",
                          out=mixedT[:, 0:4, li * 128:(li + 1) * 128],
                          in_=Mb[:, :].rearrange("p (g t) -> p g t", g=4), func=AF.Copy),
                 reads=[bank_bufs[BK_M]], writes=[buf("mixedT")])
            yield
        for g in range(4):
            bk = BK_M
            P.op("pe", I("matmul", out=banks[bk][:, 0:ntok], lhsT=wpool[:, g, :],
                         rhs=mixedT[:, g, 0:ntok], start=True, stop=True),
                 reads=[buf("wpool"), buf("mixedT")], writes=[bank_bufs[bk]])
            yield
            P.op("dve", I("tensor_scalar",
                          out=poolT[:, g, 0:ntok], in0=banks[bk][:, 0:ntok], scalar1=c_bpool[:, g:g + 1],
                          scalar2=c_pscale[:, g:g + 1], op0=ALU.add, op1=ALU.mult),
                 reads=[bank_bufs[bk], buf("c_bpool"), buf("c_pscale")], writes=[buf("poolT")])
        for li, b in enumerate(blocks):
            tiles = [(kv, ki) for kv in range(2) for ki in range(2)]

            def score(i):
                kv, ki = tiles[i]
                Kt = K0 if kv == 0 else K1
                s = (b - 1 + ki) % RING
                Sb = BK_S0 + (i % 2)
                S3 = banks[Sb][:, :].rearrange("p (c q) -> p c q", c=4)
                P.op("pe", I("matmul", out=S3, lhsT=Kt[:, s, :], rhs=qTg[:, 0:4, li * 128:(li + 1) * 128],
                             start=True, stop=False),
                     reads=[Kb[s], qT_b], writes=[bank_bufs[Sb]])
                P.op("pe", I("matmul", out=S3, lhsT=c_ident[:],
                             rhs=mk[:, ki, :].unsqueeze(1).to_broadcast([128, 4, 128]),
                             start=False, stop=True),
                     reads=[buf("ident"), buf(f"maskb{ki}")], writes=[bank_bufs[Sb]])

            def expo(i):
                Sb = BK_S0 + (i % 2)
                P.op("act", I("activation", out=Pt[i][:, :, :],
                              in_=banks[Sb][:, :].rearrange("p (c q) -> p c q", c=4), func=AF.Exp, scale=0.125),
                     reads=[bank_bufs[Sb]], writes=[buf(f"P{i}")])

            def pv(i):
                kv, ki = tiles[i]
                s = (b - 1 + ki) % RING
                Ob = BK_O0 + kv
                O3 = banks[Ob][:, :].rearrange("p (c q) -> p c q", c=4)
                P.op("pe", I("matmul", out=O3, lhsT=vaug[:, s, kv, :], rhs=Pt[i][:, :, :],
                             start=(ki == 0), stop=False),
                     reads=[Vb[s], buf(f"Vones_{s}"), buf(f"P{i}")], writes=[bank_bufs[Ob]])
                if ki == 1:
                    P.op("pe", I("matmul", out=O3, lhsT=sel[:, :],
                                 rhs=es_bf[:, kv * 4:(kv + 1) * 4].unsqueeze(2).to_broadcast([128, 4, 128]),
                                 start=False, stop=True),
                         reads=est_bufs, writes=[bank_bufs[Ob]])

            def normalise(kv):
                Ob = BK_O0 + kv
                O3 = banks[Ob][:, :].rearrange("p (c q) -> p c q", c=4)
                rd = rden[kv]
                rdb = buf(f"rden{kv}")
                P.op("act", I("activation", out=rd[64:128, :, :], in_=O3[64:128, :, :], func=AF.Ln),
                     reads=[bank_bufs[Ob]], writes=[rdb])
                P.op("act", I("activation", out=rd[64:128, :, :], in_=rd[64:128, :, :], func=AF.Exp,
                              scale=-1.0),
                     reads=[rdb], writes=[rdb])

            def finish(kv):
                Ob = BK_O0 + kv
                O3 = banks[Ob][:, :].rearrange("p (c q) -> p c q", c=4)
                rd = rden[kv]
                rdb = buf(f"rden{kv}")
                for j in range(2):
                    P.op("dve", I("tensor_tensor",
                                  out=attnT[64 * j:64 * j + 64, 2 * kv:2 * kv + 2, li * 128:(li + 1) * 128],
                                  in0=O3[0:64, j::2, :], in1=rd[64:128, j::2, :], op=ALU.mult),
                         reads=[bank_bufs[Ob], rdb], writes=[buf("attnT")])

            score(0)
            score(1)
            yield
            expo(0)
            expo(1)
            yield
            pv(0)
            score(2)
            yield
            pv(1)
            score(3)
            expo(2)
            yield
            normalise(0)
            expo(3)
            yield
            pv(2)
            pv(3)
            finish(0)
            yield
            normalise(1)
            yield
            finish(1)
            yield
        for li, b in enumerate(blocks):
            xap, xb_ = xslot(b)
            for half in range(2):
                bk = BK_S0 + half
                for kc in range(8):
                    src = attnT if kc < 4 else poolT
                    P.op("pe", I("matmul",
                                 out=banks[bk][:, :], lhsT=src[:, kc % 4, li * 128:(li + 1) * 128],
                                 rhs=wout[:, kc, half * 512:(half + 1) * 512], start=(kc == 0), stop=(kc == 7)),
                         reads=[buf("attnT"), buf("poolT"), buf("wout")], writes=[bank_bufs[bk]])
                yield
            for half in range(2):
                bk = BK_S0 + half
                P.op("dve", I("tensor_tensor",
                              out=x1[:, b - 1, half * 512:(half + 1) * 512], in0=banks[bk][:, :],
                              in1=x1[:, b - 1, half * 512:(half + 1) * 512], op=ALU.add),
                     reads=[bank_bufs[bk], xb_], writes=[xb_])
            yield

    def tail(gi):
        blocks = groups[gi]
        for b in blocks:
            xap, xb_ = xslot(b)
            c = ss_col[0]
            ss_col[0] += 1
            stats([c], [(xap, xb_)])
            yield
            yield from norm_T(xap, xb_, c, c_gffn, h2T, [h2T_b[b - 1], xhalo_b], (b - 1) * 128, "c_gffn",
                              xs2, "xs2", BK_T0)

    def drain(gen):
        for _ in gen:
            pass

    def interleave(gens):
        gens = [g for g in gens if g is not None]
        alive = list(gens)
        while alive:
            for g in list(alive):
                try:
                    next(g)
                except StopIteration:
                    alive.remove(g)

    stats([0], [xslot(0)])
    stats([1, 2], [xslot(b) for b in (1, 2)])
    drain(stage1(0))
    P.op("act", I("activation", out=c_esink[:], in_=c_esink[:], func=AF.Exp),
         reads=[], writes=[buf("c_esink")])
    P.op("dve", I("tensor_copy", out=es_bf[:, :], in_=c_esink[:, :]),
         reads=[buf("c_esink")], writes=[buf("est")])
    stats([3, 4], [xslot(b) for b in (3, 4)])
    drain(stage1(1))
    ng = len(groups)
    for gi in range(1, ng):
        interleave([stage23(gi),
                    stage1(gi + 1) if gi + 1 < ng else None,
                    tail(gi - TAIL_LAG) if gi - TAIL_LAG >= 1 else None])
    for g_ in range(max(1, ng - TAIL_LAG), ng):
        drain(tail(g_))

    final_ops = []
    if DEBUG_X1:
        if DEBUG_X1 == 2:
            allb = list(B.values())
            for t in range(NB):
                P.op("dve", I("tensor_copy", out=x1[:, t, :], in_=h2T[:, t // 2, (t % 2) * 1024:(t % 2 + 1) * 1024]),
                     reads=allb, writes=[xbuf[t]])
        for t in range(NB):
            o = dma("sp", out_d[t * 128:(t + 1) * 128, :], x1[:, t, :], reads=[xbuf[t]])
            final_ops.append(o)
    else:
        dma("sp", c_gfin[:], gfin_d.partition_broadcast(128).squeeze(1), writes=[buf("c_gfin"), buf("junk")])

        def bw(i):
            return [bank_bufs[i]]

        SCHED_LIMIT[0] = P.nseq
        P.op("pool", I("memset", c_nhalf[:], -0.5),
             reads=[], writes=[buf("c_nhalf"), hT_b, buf("hT_alt"), buf("xs"), buf("xs2"), qT_bs[0], qT_bs[1], buf("p2ok")]
             + Kb + Vb + Ub + [buf(f"Vones_{i}") for i in range(RING)] + [buf("K0z"), buf("K1z")])

        for j in range(NPASS):
            r = j % 3
            if j == 0:
                load_ffn(1)
                load_ffn(2)
            elif j + 2 < NPASS:
                load_ffn(j + 2)
            g_, u_, d_ = ffn_bufs[r]
            gb_, ub_, db_ = ffn_b[r]
            for G in range(4):
                hb = [h2T_b[G * 4 + t] for t in range(4)]
                for fc in range(2):
                    ai = (G * 2 + fc) % 4
                    si = fc
                    gbk, ubk = (0, 1) if fc == 0 else (2, 3)
                    if P0_BANKS and j == 0:
                        gbk, ubk = (1, 2) if fc == 0 else (3, 4)
                    for kc in range(8):
                        P.op("pe", I("matmul",
                            out=banks[gbk][:, :], lhsT=g_[:, kc, fc * 128:(fc + 1) * 128],
                            rhs=h2T[:, kc, G * 512:(G + 1) * 512], start=(kc == 0), stop=(kc == 7)),
                            reads=[gb_] + hb, writes=bw(gbk))
                    for kc in range(8):
                        P.op("pe", I("matmul",
                            out=banks[ubk][:, :], lhsT=u_[:, kc, fc * 128:(fc + 1) * 128],
                            rhs=h2T[:, kc, G * 512:(G + 1) * 512], start=(kc == 0), stop=(kc == 7)),
                            reads=[ub_] + hb, writes=bw(ubk))
                    P.op("act", I("activation", out=sg[si][:], in_=banks[gbk][:, :],
                                                                       func=AF.Silu),
                         reads=[bank_bufs[gbk], buf("p2ok")], writes=[buf(f"sg{si}")])
                    P.op("dve", I("tensor_tensor",
                        out=actT[ai][:], in0=banks[ubk][:, :], in1=sg[si][:], op=ALU.mult),
                        reads=[bank_bufs[ubk], buf(f"sg{si}"), buf("p2ok")], writes=[buf(f"actT{ai}")])
                for t in range(4):
                    blk = G * 4 + t
                    for half in range(2):
                        dbk = 4 + ((t * 2 + half) % 4)
                        if P0_BANKS and j == 0:
                            dbk = 5 + ((t * 2 + half) % 3)
                        for fc in range(2):
                            ai = (G * 2 + fc) % 4
                            P.op("pe", I("matmul",
                                out=banks[dbk][:, :], lhsT=actT[ai][:, t * 128:(t + 1) * 128],
                                rhs=d_[:, fc, half * 512:(half + 1) * 512], start=(fc == 0), stop=(fc == 1)),
                                reads=[db_, buf(f"actT{ai}")], writes=[bank_bufs[dbk]])
                        P.op("dve", I("tensor_tensor",
                            out=x1[:, blk, half * 512:(half + 1) * 512], in0=banks[dbk][:, :],
                            in1=x1[:, blk, half * 512:(half + 1) * 512], op=ALU.add),
                            reads=[bank_bufs[dbk], xbuf[blk]], writes=[xbuf[blk]])
                    if j == NPASS - 1:
                        c = ss_col[0] % 64
                        ss_col[0] += 1
                        sb_ = buf(f"ss{c}")
                        rb_ = buf(f"rstd{c}")
                        xap = x1[:, blk, :]
                        P.op("act", I("activation",
                            out=junk2[:], in_=xap, func=AF.Square, accum_out=ss[:, c:c + 1]),
                            reads=[xbuf[blk], buf("p2ok")], writes=[buf("junk2"), sb_])
                        P.op("act", I("activation", out=rstd[:, c:c + 1], in_=ss[:, c:c + 1], func=AF.Identity,
                                      scale=1.0 / D, bias=c_eps[:, 0:1]),
                             reads=[sb_, buf("c_eps")], writes=[rb_])
                        P.op("pool", I("tensor_tensor",
                            out=rstd[:, c:c + 1], in0=rstd[:, c:c + 1], in1=c_nhalf[:], op=ALU.pow),
                            reads=[rb_, buf("c_nhalf")], writes=[rb_])
                        if blk >= NB - 2:
                            P.op("dve", I("scalar_tensor_tensor", out=xap, in0=xap, scalar=rstd[:, c:c + 1],
                                          in1=c_gfin[:], op0=ALU.mult, op1=ALU.mult),
                                 reads=[xbuf[blk], rb_, buf("c_gfin")], writes=[xbuf[blk]])
                        else:
                            tf = tmpf[blk % 2]
                            tfb = buf(f"tmpf{blk % 2}")
                            P.op("act", I("activation", out=tf[:], in_=xap, func=AF.Copy, scale=rstd[:, c:c + 1]),
                                 reads=[xbuf[blk], rb_, buf("p2ok")], writes=[tfb])
                            P.op("pool", I("tensor_tensor", out=xap, in0=tf[:], in1=c_gfin[:], op=ALU.mult),
                                 reads=[tfb, buf("c_gfin")], writes=[xbuf[blk]])
                        o = dma("sp", out_d[blk * 128:(blk + 1) * 128, :], xap, reads=[xbuf[blk]])
                        final_ops.append(o)

    import contextlib
    with contextlib.ExitStack() as es:
        for e in ("pe", "act", "dve", "pool"):
            sems[e] = es.enter_context(nc.semaphore(f"s_{e}"))
        for q, n in (("sp", 28), ("pool", 20)):
            dma_sems[q] = [es.enter_context(nc.semaphore(f"d_{q}{i}")) for i in range(n)]
        block = es.enter_context(nc.Block())
        if USE_SCHED:
            P.schedule(limit=None)
            print("scheduler: simulated end %.1f us" % (P.sim_end / 1e3))
        P.emit(nc, block, sems, dma_sems, final_ops)
    return nc


def _prep(inputs):
    f32 = np.float32
    x = np.asarray(inputs["x"], f32)
    w_in = np.asarray(inputs["w_in"], f32)[0]
    b_in = np.asarray(inputs["b_in"], f32)[0]
    cols = []
    for c in range(4):
        A, Bh = c, c + 4
        cols += list(range(A * 64, A * 64 + 32)) + list(range(Bh * 64, Bh * 64 + 32))
        cols += list(range(A * 64 + 32, A * 64 + 64)) + list(range(Bh * 64 + 32, Bh * 64 + 64))
    cols += list(range(512, 544)) + list(range(576, 608)) + list(range(544, 576)) + list(range(608, 640))
    cols = np.array(cols)
    wqk = np.ascontiguousarray(w_in[:, cols])
    bqk = np.ascontiguousarray(b_in[cols].reshape(5, 128).T)
    bqksw = np.ascontiguousarray(np.roll(b_in[cols].reshape(5, 128), 64, axis=1).T)
    vu_cols = np.array(list(range(768, 1280)) + list(range(640, 768)))
    wvu = np.ascontiguousarray(w_in[:, vu_cols])
    bvu = np.ascontiguousarray(b_in[vu_cols].reshape(1, 640))
    common = {
        "wqk": wqk, "wvu": wvu,
        "wout": np.ascontiguousarray(np.asarray(inputs["w_out"], f32)[0]),
        "wpool": np.ascontiguousarray(np.asarray(inputs["w_pool"], f32)[0]),
        "wgate": np.ascontiguousarray(np.asarray(inputs["w_gate"], f32)[0]),
        "wup": np.ascontiguousarray(np.asarray(inputs["w_up"], f32)[0]),
        "wdown": np.ascontiguousarray(np.asarray(inputs["w_down"], f32)[0]),
        "gmix": np.ascontiguousarray(np.asarray(inputs["g_mix"], f32)[0].reshape(8, 128).T),
        "gffn": np.ascontiguousarray(np.asarray(inputs["g_ffn"], f32)[0].reshape(8, 128).T),
        "bqk": bqk, "bqksw": bqksw, "bvu": bvu,
        "bout": np.ascontiguousarray(np.asarray(inputs["b_out"], f32)[0].reshape(1, D)),
        "gfin": np.ascontiguousarray(np.asarray(inputs["g_final"], f32).reshape(1, D)),
        "bpool": np.ascontiguousarray(np.asarray(inputs["b_pool"], f32)[0].T),
        "pscale": np.ascontiguousarray(np.asarray(inputs["pool_scale"], f32)[0].T),
        "sinks": np.ascontiguousarray(np.asarray(inputs["sinks"], f32)[0].reshape(1, 8)),
        "ident": np.eye(128, dtype=f32),
    }
    sizes = (2, 4, 8, 16)
    tp = np.arange(128)[:, None]
    t = np.arange(128)[None, :]
    apool = np.zeros((16, 128, 128), f32)
    for g, s in enumerate(sizes):
        delta = t - tp
        main = ((delta >= 0) & (delta < s)).astype(f32) / s - (delta == 0).astype(f32)
        dprev = t + 128 - tp
        prev = ((dprev >= 0) & (dprev < s)).astype(f32) / s
        cnt = np.minimum(t + 1, s).astype(f32)
        mainf = ((delta >= 0) & (delta < s)).astype(f32) / cnt - (delta == 0).astype(f32)
        apool[g] = main
        apool[4 + g] = prev
        apool[8 + g] = mainf
        apool[12 + g] = 0.0
    inv_freq = (1.0 / (10000.0 ** (np.arange(0, 64, 2, dtype=f32) / f32(64)))).astype(f32)
    maps = []
    for core in range(8):
        bidx, chunk = core // 4, core % 4
        t0 = chunk * TOK
        xhh = np.zeros((17 * 128, D), f32)
        if chunk > 0:
            xhh[0:128] = x[bidx, t0 - 128:t0]
        xhh[128:] = x[bidx, t0:t0 + TOK]
        pos = (np.arange(17 * 128) + t0 - 128).astype(f32)
        ang = (pos[:, None] * inv_freq[None, :]).astype(f32)
        cosv = np.cos(ang).astype(f32).T
        sinv = np.sin(ang).astype(f32).T
        cosT = np.ascontiguousarray(np.tile(cosv, (4, 1)))
        sinT = np.ascontiguousarray(np.concatenate([-sinv, -sinv, sinv, sinv], axis=0))
        ap = apool.copy()
        if chunk > 0:
            ap[8:12] = ap[0:4]
            ap[12:16] = ap[4:8]
        m = dict(common)
        m.update({
            "xh": xhh, "cosT": cosT, "sinT": sinT, "apool": ap,
            "flag": np.full((128, 1), 0.0 if chunk == 0 else 1.0, f32),
        })
        maps.append(m)
    return maps


_NC_CACHE = {}


def kernel(**inputs):
    maps = _prep(inputs)
    if "nc" not in _NC_CACHE:
        _NC_CACHE["nc"] = build_program()
    nc = _NC_CACHE["nc"]
    res = run_bass_kernel_spmd(nc, maps, core_ids=list(range(8)))
    outs = [np.asarray(r["out"], np.float32).reshape(TOK, D) for r in res.results]
    full = np.stack([np.concatenate(outs[0:4], axis=0), np.concatenate(outs[4:8], axis=0)], axis=0)
    return full.astype(np.float32)
```

```python
import numpy as np
import concourse.bass as bass
import concourse.mybir as mybir
from concourse.bass_utils import run_bass_kernel_spmd

F32 = mybir.dt.float32
BF16 = mybir.dt.bfloat16
ALU = mybir.AluOpType
AF = mybir.ActivationFunctionType
AX = mybir.AxisListType

D = 1024
NB = 16
TOK = 2048
DFF = 2816
NPASS = 11
EPS = 1e-5
GB = 2
RING = 2 * GB + 1

DEBUG_X1 = False
USE_LN_RECIP = True
USE_SCHED = True
P0_BANKS = True
TAIL_LAG = 8
CHUNK_ORDER = [0, 1, 2, 3, 4]
SCHED_LIMIT = [None]
NO_INTERLEAVE = False


def I(method, *args, **kw):
    return (method, args, kw)


class Buf:
    __slots__ = ("name", "w", "r")

    def __init__(self, name):
        self.name = name
        self.w = None
        self.r = []


class Op:
    __slots__ = ("eng", "fn", "deps", "idx", "signal", "val", "dma", "sem", "seq")

    def __init__(self, eng, fn, deps, dma):
        self.eng = eng
        self.fn = fn
        self.deps = deps
        self.dma = dma
        self.signal = False
        self.val = None
        self.sem = None


class Prog:
    ENGS = ("pe", "act", "dve", "pool", "sp")

    def __init__(self):
        self.ops = {e: [] for e in self.ENGS}
        self.nseq = 0

    def op(self, eng, fn, reads=(), writes=(), dma=False, extra=()):
        deps = set(extra)
        for b in reads:
            if b.w is not None:
                deps.add(b.w)
        for b in writes:
            if b.w is not None:
                deps.add(b.w)
            deps.update(b.r)
        o = Op(eng, fn, deps, dma)
        o.seq = self.nseq
        self.nseq += 1
        for b in reads:
            b.r.append(o)
        for b in writes:
            b.w = o
            b.r = []
        o.idx = len(self.ops[eng])
        self.ops[eng].append(o)
        return o


    @staticmethod
    def _free(ap):
        n = 1
        for d in ap.shape[1:]:
            n *= d
        return n

    def _est(self, o):
        m, args, kw = o.fn
        out = kw.get("out", args[0] if args else None)
        n = self._free(out) if out is not None else 1
        if o.dma:
            nbytes = n * out.shape[0] * 4
            return (1000.0 if o.eng == "pool" else 120.0), 2000.0 + nbytes / 320.0
        if o.eng == "pe":
            if m == "transpose":
                n = 128
            else:
                n = self._free(kw["rhs"])
            return max(64, n) / 2.2 + 12.0, None
        if o.eng == "act":
            return 200.0 + n / 1.2 + (90.0 if kw.get("accum_out") is not None else 0.0), None
        if o.eng == "dve":
            return 190.0 + n * 1.05, None
        if o.eng == "pool":
            if m == "memset":
                return 150.0, None
            return 250.0 + n * 2.1, None
        return 100.0, None

    def schedule(self, limit=None):
        allops = [o for e in self.ENGS for o in self.ops[e]]
        rest = {e: [] for e in self.ENGS}
        if limit is not None:
            for e in self.ENGS:
                rest[e] = [o for o in self.ops[e] if o.seq >= limit]
            allops = [o for o in allops if o.seq < limit]
        order = {}
        allops.sort(key=lambda o: o.seq)
        succ = {id(o): [] for o in allops}
        indeg = {}
        for o in allops:
            indeg[id(o)] = len(o.deps)
            for d in o.deps:
                succ[id(d)].append(o)
        cp = {}
        for o in reversed(allops):
            b_, l_ = self._est(o)
            d_ = min(l_, 4000.0) if l_ is not None else b_
            cp[id(o)] = d_ + max([cp[id(s_)] + 280.0 for s_ in succ[id(o)]], default=0.0)
        fin = {}
        self._bus = 0.0
        ready = {e: [] for e in self.ENGS}
        efree = {e: 0.0 for e in self.ENGS}
        for o in allops:
            if indeg[id(o)] == 0:
                ready[o.eng].append((0.0, o.seq, o))
        new = {e: [] for e in self.ENGS}
        n_left = len(allops)
        while n_left:
            best = None
            for e in self.ENGS:
                if not ready[e]:
                    continue
                T = efree[e]
                c = min(ready[e], key=lambda r: (max(r[0], T), -cp[id(r[2])], r[1]))
                st = max(c[0], T)
                if best is None or (st, c[1]) < (best[0], best[1][1]):
                    best = (st, c, e)
            st, c, e = best
            ready[e].remove(c)
            o = c[2]
            busy, lat = self._est(o)
            efree[e] = st + busy
            if o.dma:
                xfer = lat - 2000.0
                bus = max(self._bus, st + busy) + xfer
                self._bus = bus
                fin[id(o)] = bus + 3000.0
            else:
                fin[id(o)] = st + (lat if lat is not None else busy)
            new[e].append(o)
            n_left -= 1
            for s_ in succ[id(o)]:
                indeg[id(s_)] -= 1
                if indeg[id(s_)] == 0:
                    rt = 0.0
                    for d in s_.deps:
                        hop = 60.0 if d.eng == s_.eng else 280.0
                        rt = max(rt, fin[id(d)] + hop)
                    ready[s_.eng].append((rt, s_.seq, s_))
        for e in self.ENGS:
            self.ops[e] = new[e] + rest[e]
        self.sim_end = max(fin.values())

    def emit(self, nc, block, sems, dma_sems, final_wait_ops):
        pos = {}
        for e in self.ENGS:
            for i, o in enumerate(self.ops[e]):
                pos[id(o)] = i
        for e in self.ENGS:
            for o in self.ops[e]:
                last = {}
                for d in o.deps:
                    if d.dma:
                        d.signal = True
                    elif d.eng == "pe" and o.eng == "pe" and not o.dma:
                        continue
                    elif d.eng not in last or pos[id(d)] > pos[id(last[d.eng])]:
                        last[d.eng] = d
                for d in last.values():
                    d.signal = True
        for o in final_wait_ops:
            o.signal = True
        dma_prev = {}
        for e in self.ENGS:
            cnt = 0
            ndma = 0
            for o in self.ops[e]:
                if o.dma:
                    pool = dma_sems[e]
                    k = ndma % len(pool)
                    u = ndma // len(pool)
                    o.sem = pool[k]
                    o.val = 16 * (u + 1)
                    ndma += 1
                elif o.signal:
                    cnt += 1
                    o.sem = sems[e]
                    o.val = cnt

        def run(e, eng):
            waited = {}
            for o in self.ops[e]:
                need = {}
                for d in o.deps:
                    if (not d.dma) and d.eng == "pe" and e == "pe" and not o.dma:
                        continue
                    if d.sem is None:
                        continue
                    k = d.sem
                    if d.val > need.get(k.num, (None, 0))[1]:
                        need[k.num] = (k, d.val)
                if o.dma and o.val > 16:
                    k = o.sem
                    if o.val - 16 > need.get(k.num, (None, 0))[1]:
                        need[k.num] = (k, o.val - 16)
                for num, (k, v) in need.items():
                    if waited.get(num, 0) >= v:
                        continue
                    eng.wait_ge(k, v)
                    waited[num] = v
                ins = getattr(eng, o.fn[0])(*o.fn[1], **o.fn[2])
                if o.dma:
                    ins.then_inc(o.sem, 16)
                elif o.signal:
                    ins.then_inc(o.sem, 1)
            if e == "sp":
                for o in final_wait_ops:
                    eng.wait_ge(o.sem, o.val)

        @block.tensor
        def _(eng):
            run("pe", eng)

        @block.scalar
        def _(eng):
            run("act", eng)

        @block.vector
        def _(eng):
            run("dve", eng)

        @block.gpsimd
        def _(eng):
            run("pool", eng)

        @block.sync
        def _(eng):
            run("sp", eng)


class Sbuf:
    def __init__(self, nc, base, limit):
        self.nc = nc
        self.off = base
        self.limit = limit
        self.n = 0

    def alloc(self, name, shape, dtype, at=None):
        esz = 4 if dtype == F32 else 2
        nbytes = esz
        for s in shape[1:]:
            nbytes *= s
        if at is None:
            at = self.off
            self.off = (self.off + nbytes + 31) // 32 * 32
            assert self.off <= self.limit, (name, self.off, self.limit)
        self.n += 1
        t = self.nc.alloc_sbuf_tensor_at(f"{name}_{self.n}", list(shape), dtype, offset=at)
        return t, at


def build_program():
    nc = bass.Bass("TRN2", target_bir_lowering=False)
    P = Prog()

    def dram_in(name, shape):
        return nc.dram_tensor(name, list(shape), F32, kind="ExternalInput").ap()

    xh = dram_in("xh", [17 * 128, D])
    wqk_d = dram_in("wqk", [D, 640])
    wvu_d = dram_in("wvu", [D, 640])
    wout_d = dram_in("wout", [D, D])
    wpool_d = dram_in("wpool", [4, 128, 128])
    wgate_d = dram_in("wgate", [D, DFF])
    wup_d = dram_in("wup", [D, DFF])
    wdown_d = dram_in("wdown", [DFF, D])
    gmix_d = dram_in("gmix", [128, 8])
    gffn_d = dram_in("gffn", [128, 8])
    bqk_d = dram_in("bqk", [128, 5])
    bqksw_d = dram_in("bqksw", [128, 5])
    bvu_d = dram_in("bvu", [1, 640])
    bout_d = dram_in("bout", [1, D])
    gfin_d = dram_in("gfin", [1, D])
    bpool_d = dram_in("bpool", [128, 4])
    pscale_d = dram_in("pscale", [128, 4])
    sinks_d = dram_in("sinks", [1, 8])
    flag_d = dram_in("flag", [128, 1])
    ident_d = dram_in("ident", [128, 128])
    apool_d = dram_in("apool", [16, 128, 128])
    cos_d = dram_in("cosT", [128, 17 * 128])
    sin_d = dram_in("sinT", [128, 17 * 128])
    out_d = nc.dram_tensor("out", [TOK, D], F32, kind="ExternalOutput").ap()

    total = nc.sbuf_bytes_remaining
    asize = (total - 2048) // 64 * 64
    arena = nc.alloc_sbuf_tensor("arena", [128, asize // 4], F32)
    base = nc.lookup_mloc(arena).addr
    assert base % 32 == 0
    sb = Sbuf(nc, base, base + asize)

    x1, _ = sb.alloc("x1", [128, NB, D], F32)
    h2T, h2T_off = sb.alloc("h2T", [128, 8, TOK], BF16)
    xhalo, _ = sb.alloc("xhalo", [128, D], F32, at=h2T_off)
    c_gmix, _ = sb.alloc("c_gmix", [128, 8], F32)
    c_gffn, _ = sb.alloc("c_gffn", [128, 8], F32)
    c_bqk, _ = sb.alloc("c_bqk", [128, 5], F32)
    c_bqksw, _ = sb.alloc("c_bqksw", [128, 5], F32)
    c_bpool, _ = sb.alloc("c_bpool", [128, 4], F32)
    c_pscale, _ = sb.alloc("c_pscale", [128, 4], F32)
    c_esink, _ = sb.alloc("c_esink", [128, 8], F32)
    c_flag, _ = sb.alloc("c_flag", [128, 1], F32)
    c_nhalf, _ = sb.alloc("c_nhalf", [128, 1], F32)
    c_ident, _ = sb.alloc("c_ident", [128, 128], BF16)
    c_bvu, _ = sb.alloc("c_bvu", [128, 640], F32)
    c_bout, _ = sb.alloc("c_bout", [128, D], F32)
    c_gfin, gfin_off = sb.alloc("c_gfin", [128, D], F32)
    c_eps, _ = sb.alloc("c_eps", [128, 1], F32)
    c_apool, _ = sb.alloc("c_apool", [128, 16, 128], BF16)
    ss, _ = sb.alloc("ss", [128, 64], F32)
    rstd, _ = sb.alloc("rstd", [128, 64], F32)
    ffn_bufs = []
    g0, _ = sb.alloc("gate0", [128, 8, 256], BF16)
    u0, _ = sb.alloc("up0", [128, 8, 256], BF16)
    d0, _ = sb.alloc("down0", [128, 2, D], BF16)
    ffn_bufs.append((g0, u0, d0))
    w_region = sb.off
    wqk, _ = sb.alloc("wqk", [128, 8, 640], BF16)
    wvu, _ = sb.alloc("wvu", [128, 8, 640], BF16)
    wout, _ = sb.alloc("wout", [128, 8, D], BF16)
    wpool, _ = sb.alloc("wpool", [128, 4, 128], BF16)
    o = w_region
    for i in (1, 2):
        g_, _ = sb.alloc(f"gate{i}", [128, 8, 256], BF16, at=o); o += 8 * 256 * 2
        u_, _ = sb.alloc(f"up{i}", [128, 8, 256], BF16, at=o); o += 8 * 256 * 2
        d_, _ = sb.alloc(f"down{i}", [128, 2, D], BF16, at=o); o += 2 * D * 2
        ffn_bufs.append((g_, u_, d_))
    assert o <= sb.off
    scratch = sb.off
    GT = GB * 128
    xs, _ = sb.alloc("xs", [128, D], BF16)
    junk, _ = sb.alloc("junk", [128, D], BF16, at=gfin_off)
    hT, _ = sb.alloc("hT", [128, 8, GT], BF16)
    xs2, _ = sb.alloc("xs2", [128, D], BF16)
    qT = [sb.alloc(f"qT{i}", [128, 4, GT], BF16)[0] for i in range(2)]
    K0, _ = sb.alloc("K0", [128, RING, 128], BF16)
    K1, _ = sb.alloc("K1", [128, RING, 128], BF16)
    vaug, _ = sb.alloc("vaug", [128, RING, 2, 128], BF16)
    utok, _ = sb.alloc("utok", [128, RING, 512], BF16)
    cosb = [sb.alloc(f"cos{i}", [128, GT], F32)[0] for i in range(1)]
    sinb = [sb.alloc(f"sin{i}", [128, GT], F32)[0] for i in range(1)]
    ropeA, _ = sb.alloc("ropeA", [128, GT], F32)
    ropeB, _ = sb.alloc("ropeB", [128, GT], F32)
    mixedT, _ = sb.alloc("mixedT", [128, 4, GT], BF16)
    poolT, _ = sb.alloc("poolT", [128, 4, GT], BF16)
    attnT, _ = sb.alloc("attnT", [128, 4, GT], BF16)
    Pt = [sb.alloc(f"P{i}", [128, 4, 128], BF16)[0] for i in range(4)]
    rden = [sb.alloc(f"rden{i}", [128, 4, 128], F32)[0] for i in range(2)]
    mk, _ = sb.alloc("mk", [128, 2, 128], BF16)
    es_bf, _ = sb.alloc("es_bf", [128, 8], BF16)
    sel, _ = sb.alloc("sel", [128, 128], BF16)
    hT_alt, _ = sb.alloc("hT_alt", [128, 8, GT], BF16)
    phase1_end = sb.off
    sb2 = Sbuf(nc, scratch, sb.limit)
    sb2.n = 1000
    xs_2, _ = sb2.alloc("xs_2", [128, D], BF16)
    junk_2, _ = sb2.alloc("junk_2", [128, D], BF16)
    junk2 = junk_2
    sg0_t, _ = sb2.alloc("sg0", [128, 512], F32)
    sb2.alloc("xs2_hole", [128, D], BF16)
    actT = [sb2.alloc(f"actT{i}", [128, 512], BF16)[0] for i in range(4)]
    tmpf = [sb2.alloc(f"tmpf{i}", [128, D], F32)[0] for i in range(2)]
    sg1_t, _ = sb2.alloc("sg1", [128, 512], F32)
    sg = [sg0_t, sg1_t]
    print("SBUF plan: phase1 end", phase1_end, "phase2 end", sb2.off, "limit", sb.limit)

    banks = []
    for i in range(8):
        banks.append(nc.alloc_psum_tensor(f"bank{i}", [128, 512], F32))
    bank_bufs = [Buf(f"bank{i}") for i in range(8)]

    sems = {}
    dma_sems = {}

    def dma(q, out, in_, reads=(), writes=(), extra=()):
        return P.op(q, I("dma_start", out=out, in_=in_), reads, writes, dma=True, extra=extra)

    B = {}

    def buf(name):
        if name not in B:
            B[name] = Buf(name)
        return B[name]

    xbuf = [buf(f"x1_{i}") for i in range(NB)]
    xhalo_b = buf("xhalo")

    def xslot(b):
        return (xhalo[:, :], xhalo_b) if b == 0 else (x1[:, b - 1, :], xbuf[b - 1])

    def load_x(b, extra=()):
        ap, bb = xslot(b)
        dma("sp", ap, xh[b * 128:(b + 1) * 128, :], writes=[bb], extra=extra)

    P.op("pool", I("memset", c_eps[:], EPS), writes=[buf("c_eps")])
    P.op("pool", I("memset", c_nhalf[:], -0.5), writes=[buf("c_nhalf")])
    load_x(0)
    load_x(1)
    load_x(2)
    dma("pool", wqk[:], wqk_d.rearrange("(k p) n -> p k n", p=128), writes=[buf("wqk")])
    dma("pool", c_ident[:], ident_d[:, :], writes=[buf("ident")])
    dma("sp", c_gmix[:], gmix_d[:, :], writes=[buf("c_gmix")])
    dma("sp", c_bqk[:], bqk_d[:, :], writes=[buf("c_bqk")])
    dma("sp", c_bqksw[:], bqksw_d[:, :], writes=[buf("c_bqksw")])
    dma("sp", c_bvu[:], bvu_d.partition_broadcast(128).squeeze(1), writes=[buf("c_bvu")])
    dma("sp", c_flag[:], flag_d[:, :], writes=[buf("c_flag")])
    dma("pool", wvu[:], wvu_d.rearrange("(k p) n -> p k n", p=128), writes=[buf("wvu")])
    dma("pool", c_apool[:], apool_d.rearrange("m p t -> p m t"), writes=[buf("c_apool")])
    dma("pool", wpool[:], wpool_d.rearrange("g c d -> c g d"), writes=[buf("wpool")])
    dma("sp", c_esink[:], sinks_d.partition_broadcast(128).squeeze(1), writes=[buf("c_esink")])
    dma("sp", c_bpool[:], bpool_d[:, :], writes=[buf("c_bpool")])
    dma("sp", c_pscale[:], pscale_d[:, :], writes=[buf("c_pscale")])
    dma("sp", c_bout[:], bout_d.partition_broadcast(128).squeeze(1), writes=[buf("c_bout")])
    dma("sp", c_gffn[:], gffn_d[:, :], writes=[buf("c_gffn")])
    dma("pool", wout[:], wout_d.rearrange("(k p) n -> p k n", p=128), writes=[buf("wout")])
    for b in range(3, 5):
        load_x(b)

    ffn_b = [(buf(f"ffg{i}"), buf(f"ffu{i}"), buf(f"ffd{i}")) for i in range(3)]

    def load_ffn(j):
        r = j % 3
        g_, u_, d_ = ffn_bufs[r]
        gb_, ub_, db_ = ffn_b[r]
        extra_w = [buf("wqk"), buf("wvu"), buf("wout"), buf("wpool")] if r in (1, 2) and j < 3 else []
        dma("pool", g_[:], wgate_d[:, j * 256:(j + 1) * 256].rearrange("(k p) n -> p k n", p=128),
            writes=[gb_] + extra_w)
        dma("pool", u_[:], wup_d[:, j * 256:(j + 1) * 256].rearrange("(k p) n -> p k n", p=128),
            writes=[ub_] + extra_w)
        dma("pool", d_[:], wdown_d[j * 256:(j + 1) * 256, :].rearrange("(k p) n -> p k n", p=128),
            writes=[db_] + extra_w)

    P.op("pool", I("memset", K0[:], 0.0), writes=[buf("K0z")])
    P.op("pool", I("memset", K1[:], 0.0), writes=[buf("K1z")])

    ss_col = [0]
    BK_T0, BK_QV, BK_U, BK_M, BK_S0, BK_S1, BK_O0, BK_O1 = 0, 1, 2, 3, 4, 5, 6, 7
    NEG = -30000.0
    bankQ = bankV = bank_bufs[1]

    P.op("pool", I("memset", mk[:], 0.0), writes=[buf("maskb0"), buf("maskb1")])
    P.op("pool", I("affine_select", out=mk[:, 0, :], in_=mk[:, 0, :], pattern=[[-1, 128]],
                   compare_op=ALU.is_ge, fill=NEG, base=-1, channel_multiplier=1),
         reads=[buf("maskb0")], writes=[buf("maskb0")])
    P.op("pool", I("affine_select", out=mk[:, 1, :], in_=mk[:, 1, :], pattern=[[1, 128]],
                   compare_op=ALU.is_ge, fill=NEG, base=0, channel_multiplier=-1),
         reads=[buf("maskb1")], writes=[buf("maskb1")])
    P.op("pool", I("memset", sel[:, :], 0.0), writes=[buf("est_s0")])
    P.op("pool", I("memset", sel[0:1, 64:128], 1.0), reads=[buf("est_s0")], writes=[buf("est_s1")])
    est_bufs = [buf("est"), buf("est_s0"), buf("est_s1")]

    def stats(cols, x_list):
        c0 = cols[0]
        n = len(cols)
        sbufs = []
        for c, (x_ap, x_buf) in zip(cols, x_list):
            sb_ = buf(f"ss{c}")
            sbufs.append(sb_)
            P.op("act", I("activation", out=junk[:], in_=x_ap, func=AF.Square, accum_out=ss[:, c:c + 1]),
                 reads=[x_buf], writes=[buf("junk"), sb_])
        rbs = [buf(f"rstd{c}") for c in cols]
        P.op("act", I("activation", out=rstd[:, c0:c0 + n], in_=ss[:, c0:c0 + n], func=AF.Ln,
                      scale=1.0 / D, bias=c_eps[:, 0:1]),
             reads=sbufs + [buf("c_eps")], writes=rbs)
        P.op("act", I("activation", out=rstd[:, c0:c0 + n], in_=rstd[:, c0:c0 + n], func=AF.Exp, scale=-0.5),
             reads=rbs, writes=rbs)

    def norm_T(x_ap, x_buf, c, gcol, dstT, dst_bufs, col0, tagT, xs_t, xs_name, tbank):
        rb_ = buf(f"rstd{c}")
        P.op("act", I("activation", out=xs_t[:], in_=x_ap, func=AF.Copy, scale=rstd[:, c:c + 1]),
             reads=[x_buf, rb_], writes=[buf(xs_name)])
        yield
        yield
        tb = banks[tbank]
        tbf = tb[:, :].bitcast(BF16)
        for kc in range(8):
            P.op("pe", I("transpose", out=tbf[:, kc * 128:(kc + 1) * 128],
                         in_=xs_t[:, kc * 128:(kc + 1) * 128], identity=c_ident[:]),
                 reads=[buf(xs_name), buf("ident")], writes=[bank_bufs[tbank]])
        P.op("dve", I("tensor_tensor",
                      out=dstT[:, 0:8, col0:col0 + 128],
                      in0=tbf.rearrange("p (k t) -> p k t", k=8),
                      in1=gcol[:, 0:8].unsqueeze(2).to_broadcast([128, 8, 128]), op=ALU.mult),
             reads=[bank_bufs[tbank], buf(tagT)], writes=dst_bufs)
        yield

    hT_b = buf("hT")
    hT_list = [hT, hT_alt]
    hT_bufs = [hT_b, buf("hT_alt")]
    qT_bs = [buf("qT0"), buf("qT1")]
    Kb = [buf(f"K_{i}") for i in range(RING)]
    Vb = [buf(f"V_{i}") for i in range(RING)]
    Ub = [buf(f"U_{i}") for i in range(RING)]
    tabb = [buf("tab0"), buf("tab1")]
    h2T_b = [buf(f"h2T_{i}") for i in range(NB)]

    groups = [[0]] + [list(range(1 + GB * i, 1 + GB * (i + 1))) for i in range(NB // GB)]

    ss_col[0] = 17

    def stage1(gi):
        hT = hT_list[gi % 2]
        hT_b = hT_bufs[gi % 2]
        blocks = groups[gi]
        nb = len(blocks)
        ntok = nb * 128
        b0 = blocks[0]
        halo = (b0 == 0)
        qTg = qT[gi % 2]
        qT_b = qT_bs[gi % 2]
        tb_i = 0
        t_op = dma("sp", cosb[tb_i][:, 0:ntok], cos_d[:, b0 * 128:b0 * 128 + ntok], writes=[tabb[tb_i]])
        if gi == 1:
            for b_ in range(5, 17):
                load_x(b_, extra=[t_op])
            stats(list(range(5, 11)), [xslot(b_) for b_ in range(5, 11)])
            stats(list(range(11, 17)), [xslot(b_) for b_ in range(11, 17)])
        if gi == 2:
            load_ffn(0)
        dma("sp", sinb[tb_i][:, 0:ntok], sin_d[:, b0 * 128:b0 * 128 + ntok], writes=[tabb[tb_i]])
        for li, b in enumerate(blocks):
            xap, xb_ = xslot(b)
            yield from norm_T(xap, xb_, b, c_gmix, hT, [hT_b], li * 128, "c_gmix", xs, "xs", BK_T0)
            if not halo:
                P.op("pool", I("tensor_tensor", out=xap, in0=xap, in1=c_bout[:], op=ALU.add),
                     reads=[xb_, buf("c_bout")], writes=[xb_])
        for li, b in enumerate(blocks):
            s = b % RING
            for kc in range(8):
                P.op("pe", I("matmul", out=banks[BK_U][:, :], lhsT=hT[:, kc, li * 128:(li + 1) * 128],
                             rhs=wvu[:, kc, 0:512], start=(kc == 0), stop=(kc == 7)),
                     reads=[buf("wvu"), hT_b], writes=[bank_bufs[BK_U]])
            for kc in range(8):
                P.op("pe", I("matmul", out=banks[BK_QV][:, 256:384], lhsT=hT[:, kc, li * 128:(li + 1) * 128],
                             rhs=wvu[:, kc, 512:640], start=(kc == 0), stop=(kc == 7)),
                     reads=[buf("wvu"), hT_b], writes=[bankV])
            yield
            yield
            P.op("dve", I("tensor_tensor", out=utok[:, s, :], in0=banks[BK_U][:, :], in1=c_bvu[:, 0:512],
                          op=ALU.add),
                 reads=[bank_bufs[BK_U], buf("c_bvu")], writes=[Ub[s]])
            P.op("dve", I("tensor_tensor",
                          out=vaug[:, s, :, 0:64],
                          in0=banks[BK_QV][:, 256:384].rearrange("p (k d) -> p k d", k=2),
                          in1=c_bvu[:, 512:640].rearrange("p (k d) -> p k d", k=2), op=ALU.add),
                 reads=[bankV, buf("c_bvu")], writes=[Vb[s]])
            P.op("pool", I("memset", vaug[:, s, :, 64:128], 1.0), reads=[], writes=[buf(f"Vones_{s}")])
            if b == 0:
                P.op("dve", I("tensor_scalar", out=vaug[:, s, :, :], in0=vaug[:, s, :, :],
                              scalar1=c_flag[:, 0:1], scalar2=None, op0=ALU.mult),
                     reads=[Vb[s], buf(f"Vones_{s}"), buf("c_flag")], writes=[Vb[s], buf(f"Vones_{s}")])
            yield
        chunks = [4] if halo else CHUNK_ORDER
        for ci, c in enumerate(chunks):
            for kc in range(8):
                P.op("pe", I("matmul", out=banks[BK_QV][:, 0:ntok], lhsT=wqk[:, kc, c * 128:(c + 1) * 128],
                             rhs=hT[:, kc, 0:ntok], start=(kc == 0), stop=(kc == 7)),
                     reads=[buf("wqk"), hT_b], writes=[bankQ])
            yield
            yield
            Z = banks[BK_QV]
            P.op("dve", I("scalar_tensor_tensor",
                          out=ropeA[:, 0:ntok], in0=Z[:, 0:ntok], scalar=c_bqk[:, c:c + 1],
                          in1=cosb[tb_i][:, 0:ntok], op0=ALU.add, op1=ALU.mult),
                 reads=[bankQ, buf("c_bqk"), tabb[tb_i]], writes=[buf("ropeA")])
            P.op("dve", I("scalar_tensor_tensor",
                          out=ropeB[0:64, 0:ntok], in0=Z[64:128, 0:ntok], scalar=c_bqksw[0:64, c:c + 1],
                          in1=sinb[tb_i][0:64, 0:ntok], op0=ALU.add, op1=ALU.mult),
                 reads=[bankQ, buf("c_bqksw"), tabb[tb_i]], writes=[buf("ropeB0")])
            P.op("dve", I("scalar_tensor_tensor",
                          out=ropeB[64:128, 0:ntok], in0=Z[0:64, 0:ntok], scalar=c_bqksw[64:128, c:c + 1],
                          in1=sinb[tb_i][64:128, 0:ntok], op0=ALU.add, op1=ALU.mult),
                 reads=[bankQ, buf("c_bqksw"), tabb[tb_i]], writes=[buf("ropeB1")])
            rb = [buf("ropeA"), buf("ropeB0"), buf("ropeB1")]
            if c < 4:
                P.op("pool", I("tensor_tensor", out=qTg[:, c, 0:ntok], in0=ropeA[:, 0:ntok],
                               in1=ropeB[:, 0:ntok], op=ALU.add),
                     reads=rb, writes=[qT_b])
            else:
                for li, b in enumerate(blocks):
                    s = b % RING
                    for (q0, Kt) in ((0, K0), (32, K1), (64, K0), (96, K1)):
                        P.op("pool", I("tensor_tensor",
                                       out=Kt[q0:q0 + 32, s, :], in0=ropeA[q0:q0 + 32, li * 128:(li + 1) * 128],
                                       in1=ropeB[q0:q0 + 32, li * 128:(li + 1) * 128], op=ALU.add),
                             reads=rb + [buf("K0z"), buf("K1z")], writes=[Kb[s]])
            yield

    def stage23(gi):
        blocks = groups[gi]
        nb = len(blocks)
        ntok = nb * 128
        tb_i = gi % 2
        qTg = qT[tb_i]
        qT_b = qT_bs[tb_i]
        for li, b in enumerate(blocks):
            s, sp_ = b % RING, (b - 1) % RING
            first = 8 if b == 1 else 0
            Mb = banks[BK_M]
            for g in range(4):
                P.op("pe", I("matmul", out=Mb[:, g * 128:(g + 1) * 128], lhsT=utok[:, s, g * 128:(g + 1) * 128],
                             rhs=c_apool[:, first + g, :], start=True, stop=False),
                     reads=[Ub[s], buf("c_apool")], writes=[bank_bufs[BK_M]])
                P.op("pe", I("matmul", out=Mb[:, g * 128:(g + 1) * 128], lhsT=utok[:, sp_, g * 128:(g + 1) * 128],
                             rhs=c_apool[:, first + 4 + g, :], start=False, stop=True),
                     reads=[Ub[sp_], buf("c_apool")], writes=[bank_bufs[BK_M]])
            yield
            P.op("act", I("activation",
                          out=mixedT[:, 0:4, li * 128:(li + 1) * 128],
                          in_=Mb[:, :].rearrange("p (g t) -> p g t", g=4), func=AF.Copy),
                 reads=[bank_bufs[BK_M]], writes=[buf("mixedT")])
            yield
        for g in range(4):
            bk = BK_M
            P.op("pe", I("matmul", out=banks[bk][:, 0:ntok], lhsT=wpool[:, g, :],
                         rhs=mixedT[:, g, 0:ntok], start=True, stop=True),
                 reads=[buf("wpool"), buf("mixedT")], writes=[bank_bufs[bk]])
            yield
            P.op("dve", I("tensor_scalar",
                          out=poolT[:, g, 0:ntok], in0=banks[bk][:, 0:ntok], scalar1=c_bpool[:, g:g + 1],
                          scalar2=c_pscale[:, g:g + 1], op0=ALU.add, op1=ALU.mult),
                 reads=[bank_bufs[bk], buf("c_bpool"), buf("c_pscale")], writes=[buf("poolT")])
        for li, b in enumerate(blocks):
            tiles = [(kv, ki) for kv in range(2) for ki in range(2)]

            def score(i):
                kv, ki = tiles[i]
                Kt = K0 if kv == 0 else K1
                s = (b - 1 + ki) % RING
                Sb = BK_S0 + (i % 2)
                S3 = banks[Sb][:, :].rearrange("p (c q) -> p c q", c=4)
                P.op("pe", I("matmul", out=S3, lhsT=Kt[:, s, :], rhs=qTg[:, 0:4, li * 128:(li + 1) * 128],
                             start=True, stop=False),
                     reads=[Kb[s], qT_b], writes=[bank_bufs[Sb]])
                P.op("pe", I("matmul", out=S3, lhsT=c_ident[:],
                             rhs=mk[:, ki, :].unsqueeze(1).to_broadcast([128, 4, 128]),
                             start=False, stop=True),
                     reads=[buf("ident"), buf(f"maskb{ki}")], writes=[bank_bufs[Sb]])

            def expo(i):
                Sb = BK_S0 + (i % 2)
                P.op("act", I("activation", out=Pt[i][:, :, :],
                              in_=banks[Sb][:, :].rearrange("p (c q) -> p c q", c=4), func=AF.Exp, scale=0.125),
                     reads=[bank_bufs[Sb]], writes=[buf(f"P{i}")])

            def pv(i):
                kv, ki = tiles[i]
                s = (b - 1 + ki) % RING
                Ob = BK_O0 + kv
                O3 = banks[Ob][:, :].rearrange("p (c q) -> p c q", c=4)
                P.op("pe", I("matmul", out=O3, lhsT=vaug[:, s, kv, :], rhs=Pt[i][:, :, :],
                             start=(ki == 0), stop=False),
                     reads=[Vb[s], buf(f"Vones_{s}"), buf(f"P{i}")], writes=[bank_bufs[Ob]])
                if ki == 1:
                    P.op("pe", I("matmul", out=O3, lhsT=sel[:, :],
                                 rhs=es_bf[:, kv * 4:(kv + 1) * 4].unsqueeze(2).to_broadcast([128, 4, 128]),
                                 start=False, stop=True),
                         reads=est_bufs, writes=[bank_bufs[Ob]])

            def normalise(kv):
                Ob = BK_O0 + kv
                O3 = banks[Ob][:, :].rearrange("p (c q) -> p c q", c=4)
                rd = rden[kv]
                rdb = buf(f"rden{kv}")
                P.op("act", I("activation", out=rd[64:128, :, :], in_=O3[64:128, :, :], func=AF.Ln),
                     reads=[bank_bufs[Ob]], writes=[rdb])
                P.op("act", I("activation", out=rd[64:128, :, :], in_=rd[64:128, :, :], func=AF.Exp,
                              scale=-1.0),
                     reads=[rdb], writes=[rdb])

            def finish(kv):
                Ob = BK_O0 + kv
                O3 = banks[Ob][:, :].rearrange("p (c q) -> p c q", c=4)
                rd = rden[kv]
                rdb = buf(f"rden{kv}")
                for j in range(2):
                    P.op("dve", I("tensor_tensor",
                                  out=attnT[64 * j:64 * j + 64, 2 * kv:2 * kv + 2, li * 128:(li + 1) * 128],
                                  in0=O3[0:64, j::2, :], in1=rd[64:128, j::2, :], op=ALU.mult),
                         reads=[bank_bufs[Ob], rdb], writes=[buf("attnT")])

            score(0)
            score(1)
            yield
            expo(0)
            expo(1)
            yield
            pv(0)
            score(2)
            yield
            pv(1)
            score(3)
            expo(2)
            yield
            normalise(0)
            expo(3)
            yield
            pv(2)
            pv(3)
            finish(0)
            yield
            normalise(1)
            yield
            finish(1)
            yield
        for li, b in enumerate(blocks):
            xap, xb_ = xslot(b)
            for half in range(2):
                bk = BK_S0 + half
                for kc in range(8):
                    src = attnT if kc < 4 else poolT
                    P.op("pe", I("matmul",
                                 out=banks[bk][:, :], lhsT=src[:, kc % 4, li * 128:(li + 1) * 128],
                                 rhs=wout[:, kc, half * 512:(half + 1) * 512], start=(kc == 0), stop=(kc == 7)),
                         reads=[buf("attnT"), buf("poolT"), buf("wout")], writes=[bank_bufs[bk]])
                yield
            for half in range(2):
                bk = BK_S0 + half
                P.op("dve", I("tensor_tensor",
                              out=x1[:, b - 1, half * 512:(half + 1) * 512], in0=banks[bk][:, :],
                              in1=x1[:, b - 1, half * 512:(half + 1) * 512], op=ALU.add),
                     reads=[bank_bufs[bk], xb_], writes=[xb_])
            yield

    def tail(gi):
        blocks = groups[gi]
        for b in blocks:
            xap, xb_ = xslot(b)
            c = ss_col[0]
            ss_col[0] += 1
            stats([c], [(xap, xb_)])
            yield
            yield from norm_T(xap, xb_, c, c_gffn, h2T, [h2T_b[b - 1], xhalo_b], (b - 1) * 128, "c_gffn",
                              xs2, "xs2", BK_T0)

    def drain(gen):
        for _ in gen:
            pass

    def interleave(gens):
        gens = [g for g in gens if g is not None]
        alive = list(gens)
        while alive:
            for g in list(alive):
                try:
                    next(g)
                except StopIteration:
                    alive.remove(g)

    stats([0, 1, 2], [xslot(b) for b in (0, 1, 2)])
    drain(stage1(0))
    P.op("act", I("activation", out=c_esink[:], in_=c_esink[:], func=AF.Exp),
         reads=[], writes=[buf("c_esink")])
    P.op("dve", I("tensor_copy", out=es_bf[:, :], in_=c_esink[:, :]),
         reads=[buf("c_esink")], writes=[buf("est")])
    stats([3, 4], [xslot(b) for b in (3, 4)])
    drain(stage1(1))
    ng = len(groups)
    for gi in range(1, ng):
        interleave([stage23(gi),
                    stage1(gi + 1) if gi + 1 < ng else None,
                    tail(gi - TAIL_LAG) if gi - TAIL_LAG >= 1 else None])
    for g_ in range(max(1, ng - TAIL_LAG), ng):
        drain(tail(g_))

    final_ops = []
    if DEBUG_X1:
        if DEBUG_X1 == 2:
            allb = list(B.values())
            for t in range(NB):
                P.op("dve", I("tensor_copy", out=x1[:, t, :], in_=h2T[:, t // 2, (t % 2) * 1024:(t % 2 + 1) * 1024]),
                     reads=allb, writes=[xbuf[t]])
        for t in range(NB):
            o = dma("sp", out_d[t * 128:(t + 1) * 128, :], x1[:, t, :], reads=[xbuf[t]])
            final_ops.append(o)
    else:
        dma("sp", c_gfin[:], gfin_d.partition_broadcast(128).squeeze(1), writes=[buf("c_gfin"), buf("junk")])

        def bw(i):
            return [bank_bufs[i]]

        SCHED_LIMIT[0] = P.nseq
        P.op("pool", I("memset", c_nhalf[:], -0.5),
             reads=[], writes=[buf("c_nhalf"), hT_b, buf("hT_alt"), qT_bs[0], qT_bs[1], buf("p2ok")]
             + Kb + Vb + Ub + [buf(f"Vones_{i}") for i in range(RING)] + [buf("K0z"), buf("K1z")])

        for j in range(NPASS):
            r = j % 3
            if j == 0:
                load_ffn(1)
                load_ffn(2)
            elif j + 2 < NPASS:
                load_ffn(j + 2)
            g_, u_, d_ = ffn_bufs[r]
            gb_, ub_, db_ = ffn_b[r]
            for G in range(4):
                hb = [h2T_b[G * 4 + t] for t in range(4)]
                for fc in range(2):
                    ai = (G * 2 + fc) % 4
                    si = fc
                    gbk, ubk = (0, 1) if fc == 0 else (2, 3)
                    if P0_BANKS and j == 0:
                        gbk, ubk = (1, 2) if fc == 0 else (3, 4)
                    for kc in range(8):
                        P.op("pe", I("matmul",
                            out=banks[gbk][:, :], lhsT=g_[:, kc, fc * 128:(fc + 1) * 128],
                            rhs=h2T[:, kc, G * 512:(G + 1) * 512], start=(kc == 0), stop=(kc == 7)),
                            reads=[gb_] + hb, writes=bw(gbk))
                    for kc in range(8):
                        P.op("pe", I("matmul",
                            out=banks[ubk][:, :], lhsT=u_[:, kc, fc * 128:(fc + 1) * 128],
                            rhs=h2T[:, kc, G * 512:(G + 1) * 512], start=(kc == 0), stop=(kc == 7)),
                            reads=[ub_] + hb, writes=bw(ubk))
                    P.op("act", I("activation", out=sg[si][:], in_=banks[gbk][:, :],
                                                                       func=AF.Silu),
                         reads=[bank_bufs[gbk], buf("p2ok")], writes=[buf(f"sg{si}")])
                    P.op("dve", I("tensor_tensor",
                        out=actT[ai][:], in0=banks[ubk][:, :], in1=sg[si][:], op=ALU.mult),
                        reads=[bank_bufs[ubk], buf(f"sg{si}"), buf("p2ok")], writes=[buf(f"actT{ai}")])
                for t in range(4):
                    blk = G * 4 + t
                    for half in range(2):
                        dbk = 4 + ((t * 2 + half) % 4)
                        if P0_BANKS and j == 0:
                            dbk = 5 + ((t * 2 + half) % 3)
                        for fc in range(2):
                            ai = (G * 2 + fc) % 4
                            P.op("pe", I("matmul",
                                out=banks[dbk][:, :], lhsT=actT[ai][:, t * 128:(t + 1) * 128],
                                rhs=d_[:, fc, half * 512:(half + 1) * 512], start=(fc == 0), stop=(fc == 1)),
                                reads=[db_, buf(f"actT{ai}")], writes=[bank_bufs[dbk]])
                        P.op("dve", I("tensor_tensor",
                            out=x1[:, blk, half * 512:(half + 1) * 512], in0=banks[dbk][:, :],
                            in1=x1[:, blk, half * 512:(half + 1) * 512], op=ALU.add),
                            reads=[bank_bufs[dbk], xbuf[blk]], writes=[xbuf[blk]])
                    if j == NPASS - 1:
                        c = ss_col[0] % 64
                        ss_col[0] += 1
                        sb_ = buf(f"ss{c}")
                        rb_ = buf(f"rstd{c}")
                        xap = x1[:, blk, :]
                        P.op("act", I("activation",
                            out=junk2[:], in_=xap, func=AF.Square, accum_out=ss[:, c:c + 1]),
                            reads=[xbuf[blk], buf("p2ok")], writes=[buf("junk2"), sb_])
                        P.op("act", I("activation", out=rstd[:, c:c + 1], in_=ss[:, c:c + 1], func=AF.Identity,
                                      scale=1.0 / D, bias=c_eps[:, 0:1]),
                             reads=[sb_, buf("c_eps")], writes=[rb_])
                        P.op("pool", I("tensor_tensor",
                            out=rstd[:, c:c + 1], in0=rstd[:, c:c + 1], in1=c_nhalf[:], op=ALU.pow),
                            reads=[rb_, buf("c_nhalf")], writes=[rb_])
                        if blk >= NB - 2:
                            P.op("dve", I("scalar_tensor_tensor", out=xap, in0=xap, scalar=rstd[:, c:c + 1],
                                          in1=c_gfin[:], op0=ALU.mult, op1=ALU.mult),
                                 reads=[xbuf[blk], rb_, buf("c_gfin")], writes=[xbuf[blk]])
                        else:
                            tf = tmpf[blk % 2]
                            tfb = buf(f"tmpf{blk % 2}")
                            P.op("act", I("activation", out=tf[:], in_=xap, func=AF.Copy, scale=rstd[:, c:c + 1]),
                                 reads=[xbuf[blk], rb_, buf("p2ok")], writes=[tfb])
                            P.op("pool", I("tensor_tensor", out=xap, in0=tf[:], in1=c_gfin[:], op=ALU.mult),
                                 reads=[tfb, buf("c_gfin")], writes=[xbuf[blk]])
                        o = dma("sp", out_d[blk * 128:(blk + 1) * 128, :], xap, reads=[xbuf[blk]])
                        final_ops.append(o)

    import contextlib
    with contextlib.ExitStack() as es:
        for e in ("pe", "act", "dve", "pool"):
            sems[e] = es.enter_context(nc.semaphore(f"s_{e}"))
        for q, n in (("sp", 28), ("pool", 20)):
            dma_sems[q] = [es.enter_context(nc.semaphore(f"d_{q}{i}")) for i in range(n)]
        block = es.enter_context(nc.Block())
        if USE_SCHED:
            P.schedule(limit=None)
            print("scheduler: simulated end %.1f us" % (P.sim_end / 1e3))
        P.emit(nc, block, sems, dma_sems, final_ops)
    return nc


def _prep(inputs):
    f32 = np.float32
    x = np.asarray(inputs["x"], f32)
    w_in = np.asarray(inputs["w_in"], f32)[0]
    b_in = np.asarray(inputs["b_in"], f32)[0]
    cols = []
    for c in range(4):
        A, Bh = c, c + 4
        cols += list(range(A * 64, A * 64 + 32)) + list(range(Bh * 64, Bh * 64 + 32))
        cols += list(range(A * 64 + 32, A * 64 + 64)) + list(range(Bh * 64 + 32, Bh * 64 + 64))
    cols += list(range(512, 544)) + list(range(576, 608)) + list(range(544, 576)) + list(range(608, 640))
    cols = np.array(cols)
    wqk = np.ascontiguousarray(w_in[:, cols])
    bqk = np.ascontiguousarray(b_in[cols].reshape(5, 128).T)
    bqksw = np.ascontiguousarray(np.roll(b_in[cols].reshape(5, 128), 64, axis=1).T)
    vu_cols = np.array(list(range(768, 1280)) + list(range(640, 768)))
    wvu = np.ascontiguousarray(w_in[:, vu_cols])
    bvu = np.ascontiguousarray(b_in[vu_cols].reshape(1, 640))
    common = {
        "wqk": wqk, "wvu": wvu,
        "wout": np.ascontiguousarray(np.asarray(inputs["w_out"], f32)[0]),
        "wpool": np.ascontiguousarray(np.asarray(inputs["w_pool"], f32)[0]),
        "wgate": np.ascontiguousarray(np.asarray(inputs["w_gate"], f32)[0]),
        "wup": np.ascontiguousarray(np.asarray(inputs["w_up"], f32)[0]),
        "wdown": np.ascontiguousarray(np.asarray(inputs["w_down"], f32)[0]),
        "gmix": np.ascontiguousarray(np.asarray(inputs["g_mix"], f32)[0].reshape(8, 128).T),
        "gffn": np.ascontiguousarray(np.asarray(inputs["g_ffn"], f32)[0].reshape(8, 128).T),
        "bqk": bqk, "bqksw": bqksw, "bvu": bvu,
        "bout": np.ascontiguousarray(np.asarray(inputs["b_out"], f32)[0].reshape(1, D)),
        "gfin": np.ascontiguousarray(np.asarray(inputs["g_final"], f32).reshape(1, D)),
        "bpool": np.ascontiguousarray(np.asarray(inputs["b_pool"], f32)[0].T),
        "pscale": np.ascontiguousarray(np.asarray(inputs["pool_scale"], f32)[0].T),
        "sinks": np.ascontiguousarray(np.asarray(inputs["sinks"], f32)[0].reshape(1, 8)),
        "ident": np.eye(128, dtype=f32),
    }
    sizes = (2, 4, 8, 16)
    tp = np.arange(128)[:, None]
    t = np.arange(128)[None, :]
    apool = np.zeros((16, 128, 128), f32)
    for g, s in enumerate(sizes):
        delta = t - tp
        main = ((delta >= 0) & (delta < s)).astype(f32) / s - (delta == 0).astype(f32)
        dprev = t + 128 - tp
        prev = ((dprev >= 0) & (dprev < s)).astype(f32) / s
        cnt = np.minimum(t + 1, s).astype(f32)
        mainf = ((delta >= 0) & (delta < s)).astype(f32) / cnt - (delta == 0).astype(f32)
        apool[g] = main
        apool[4 + g] = prev
        apool[8 + g] = mainf
        apool[12 + g] = 0.0
    inv_freq = (1.0 / (10000.0 ** (np.arange(0, 64, 2, dtype=f32) / f32(64)))).astype(f32)
    maps = []
    for core in range(8):
        bidx, chunk = core // 4, core % 4
        t0 = chunk * TOK
        xhh = np.zeros((17 * 128, D), f32)
        if chunk > 0:
            xhh[0:128] = x[bidx, t0 - 128:t0]
        xhh[128:] = x[bidx, t0:t0 + TOK]
        pos = (np.arange(17 * 128) + t0 - 128).astype(f32)
        ang = (pos[:, None] * inv_freq[None, :]).astype(f32)
        cosv = np.cos(ang).astype(f32).T
        sinv = np.sin(ang).astype(f32).T
        cosT = np.ascontiguousarray(np.tile(cosv, (4, 1)))
        sinT = np.ascontiguousarray(np.concatenate([-sinv, -sinv, sinv, sinv], axis=0))
        ap = apool.copy()
        if chunk > 0:
            ap[8:12] = ap[0:4]
            ap[12:16] = ap[4:8]
        m = dict(common)
        m.update({
            "xh": xhh, "cosT": cosT, "sinT": sinT, "apool": ap,
            "flag": np.full((128, 1), 0.0 if chunk == 0 else 1.0, f32),
        })
        maps.append(m)
    return maps


_NC_CACHE = {}


def kernel(**inputs):
    maps = _prep(inputs)
    if "nc" not in _NC_CACHE:
        _NC_CACHE["nc"] = build_program()
    nc = _NC_CACHE["nc"]
    res = run_bass_kernel_spmd(nc, maps, core_ids=list(range(8)))
    outs = [np.asarray(r["out"], np.float32).reshape(TOK, D) for r in res.results]
    full = np.stack([np.concatenate(outs[0:4], axis=0), np.concatenate(outs[4:8], axis=0)], axis=0)
    return full.astype(np.float32)
```

```python
import numpy as np
import concourse.bass as bass
import concourse.mybir as mybir
from concourse.bass_utils import run_bass_kernel_spmd

F32 = mybir.dt.float32
BF16 = mybir.dt.bfloat16
ALU = mybir.AluOpType
AF = mybir.ActivationFunctionType
AX = mybir.AxisListType

D = 1024
NB = 16
TOK = 2048
DFF = 2816
NPASS = 11
EPS = 1e-5
GB = 2
RING = 2 * GB + 1

DEBUG_X1 = False
USE_LN_RECIP = True
USE_SCHED = True
P0_BANKS = True
TAIL_LAG = 8
CHUNK_ORDER = [0, 1, 2, 3, 4]
SCHED_LIMIT = [None]
NO_INTERLEAVE = False


def I(method, *args, **kw):
    return (method, args, kw)


class Buf:
    __slots__ = ("name", "w", "r")

    def __init__(self, name):
        self.name = name
        self.w = None
        self.r = []


class Op:
    __slots__ = ("eng", "fn", "deps", "idx", "signal", "val", "dma", "sem", "seq")

    def __init__(self, eng, fn, deps, dma):
        self.eng = eng
        self.fn = fn
        self.deps = deps
        self.dma = dma
        self.signal = False
        self.val = None
        self.sem = None


class Prog:
    ENGS = ("pe", "act", "dve", "pool", "sp")

    def __init__(self):
        self.ops = {e: [] for e in self.ENGS}
        self.nseq = 0

    def op(self, eng, fn, reads=(), writes=(), dma=False, extra=()):
        deps = set(extra)
        for b in reads:
            if b.w is not None:
                deps.add(b.w)
        for b in writes:
            if b.w is not None:
                deps.add(b.w)
            deps.update(b.r)
        o = Op(eng, fn, deps, dma)
        o.seq = self.nseq
        self.nseq += 1
        for b in reads:
            b.r.append(o)
        for b in writes:
            b.w = o
            b.r = []
        o.idx = len(self.ops[eng])
        self.ops[eng].append(o)
        return o


    @staticmethod
    def _free(ap):
        n = 1
        for d in ap.shape[1:]:
            n *= d
        return n

    def _est(self, o):
        m, args, kw = o.fn
        out = kw.get("out", args[0] if args else None)
        n = self._free(out) if out is not None else 1
        if o.dma:
            nbytes = n * out.shape[0] * 4
            return (1000.0 if o.eng == "pool" else 120.0), 2000.0 + nbytes / 320.0
        if o.eng == "pe":
            if m == "transpose":
                n = 128
            else:
                n = self._free(kw["rhs"])
            return max(64, n) / 2.2 + 12.0, None
        if o.eng == "act":
            return 200.0 + n / 1.2 + (90.0 if kw.get("accum_out") is not None else 0.0), None
        if o.eng == "dve":
            return 190.0 + n * 1.05, None
        if o.eng == "pool":
            if m == "memset":
                return 150.0, None
            return 250.0 + n * 2.1, None
        return 100.0, None

    def schedule(self, limit=None):
        allops = [o for e in self.ENGS for o in self.ops[e]]
        rest = {e: [] for e in self.ENGS}
        if limit is not None:
            for e in self.ENGS:
                rest[e] = [o for o in self.ops[e] if o.seq >= limit]
            allops = [o for o in allops if o.seq < limit]
        order = {}
        allops.sort(key=lambda o: o.seq)
        succ = {id(o): [] for o in allops}
        indeg = {}
        for o in allops:
            indeg[id(o)] = len(o.deps)
            for d in o.deps:
                succ[id(d)].append(o)
        cp = {}
        for o in reversed(allops):
            b_, l_ = self._est(o)
            d_ = min(l_, 4000.0) if l_ is not None else b_
            cp[id(o)] = d_ + max([cp[id(s_)] + 280.0 for s_ in succ[id(o)]], default=0.0)
        fin = {}
        self._bus = 0.0
        ready = {e: [] for e in self.ENGS}
        efree = {e: 0.0 for e in self.ENGS}
        for o in allops:
            if indeg[id(o)] == 0:
                ready[o.eng].append((0.0, o.seq, o))
        new = {e: [] for e in self.ENGS}
        n_left = len(allops)
        while n_left:
            best = None
            for e in self.ENGS:
                if not ready[e]:
                    continue
                T = efree[e]
                c = min(ready[e], key=lambda r: (max(r[0], T), -cp[id(r[2])], r[1]))
                st = max(c[0], T)
                if best is None or (st, c[1]) < (best[0], best[1][1]):
                    best = (st, c, e)
            st, c, e = best
            ready[e].remove(c)
            o = c[2]
            busy, lat = self._est(o)
            efree[e] = st + busy
            if o.dma:
                xfer = lat - 2000.0
                bus = max(self._bus, st + busy) + xfer
                self._bus = bus
                fin[id(o)] = bus + 3000.0
            else:
                fin[id(o)] = st + (lat if lat is not None else busy)
            new[e].append(o)
            n_left -= 1
            for s_ in succ[id(o)]:
                indeg[id(s_)] -= 1
                if indeg[id(s_)] == 0:
                    rt = 0.0
                    for d in s_.deps:
                        hop = 60.0 if d.eng == s_.eng else 280.0
                        rt = max(rt, fin[id(d)] + hop)
                    ready[s_.eng].append((rt, s_.seq, s_))
        for e in self.ENGS:
            self.ops[e] = new[e] + rest[e]
        self.sim_end = max(fin.values())

    def emit(self, nc, block, sems, dma_sems, final_wait_ops):
        pos = {}
        for e in self.ENGS:
            for i, o in enumerate(self.ops[e]):
                pos[id(o)] = i
        for e in self.ENGS:
            for o in self.ops[e]:
                last = {}
                for d in o.deps:
                    if d.dma:
                        d.signal = True
                    elif d.eng == "pe" and o.eng == "pe" and not o.dma:
                        continue
                    elif d.eng not in last or pos[id(d)] > pos[id(last[d.eng])]:
                        last[d.eng] = d
                for d in last.values():
                    d.signal = True
        for o in final_wait_ops:
            o.signal = True
        dma_prev = {}
        for e in self.ENGS:
            cnt = 0
            ndma = 0
            for o in self.ops[e]:
                if o.dma:
                    pool = dma_sems[e]
                    k = ndma % len(pool)
                    u = ndma // len(pool)
                    o.sem = pool[k]
                    o.val = 16 * (u + 1)
                    ndma += 1
                elif o.signal:
                    cnt += 1
                    o.sem = sems[e]
                    o.val = cnt

        def run(e, eng):
            waited = {}
            for o in self.ops[e]:
                need = {}
                for d in o.deps:
                    if (not d.dma) and d.eng == "pe" and e == "pe" and not o.dma:
                        continue
                    if d.sem is None:
                        continue
                    k = d.sem
                    if d.val > need.get(k.num, (None, 0))[1]:
                        need[k.num] = (k, d.val)
                if o.dma and o.val > 16:
                    k = o.sem
                    if o.val - 16 > need.get(k.num, (None, 0))[1]:
                        need[k.num] = (k, o.val - 16)
                for num, (k, v) in need.items():
                    if waited.get(num, 0) >= v:
                        continue
                    eng.wait_ge(k, v)
                    waited[num] = v
                ins = getattr(eng, o.fn[0])(*o.fn[1], **o.fn[2])
                if o.dma:
                    ins.then_inc(o.sem, 16)
                elif o.signal:
                    ins.then_inc(o.sem, 1)
            if e == "sp":
                for o in final_wait_ops:
                    eng.wait_ge(o.sem, o.val)

        @block.tensor
        def _(eng):
            run("pe", eng)

        @block.scalar
        def _(eng):
            run("act", eng)

        @block.vector
        def _(eng):
            run("dve", eng)

        @block.gpsimd
        def _(eng):
            run("pool", eng)

        @block.sync
        def _(eng):
            run("sp", eng)


class Sbuf:
    def __init__(self, nc, base, limit):
        self.nc = nc
        self.off = base
        self.limit = limit
        self.n = 0

    def alloc(self, name, shape, dtype, at=None):
        esz = 4 if dtype == F32 else 2
        nbytes = esz
        for s in shape[1:]:
            nbytes *= s
        if at is None:
            at = self.off
            self.off = (self.off + nbytes + 31) // 32 * 32
            assert self.off <= self.limit, (name, self.off, self.limit)
        self.n += 1
        t = self.nc.alloc_sbuf_tensor_at(f"{name}_{self.n}", list(shape), dtype, offset=at)
        return t, at


def build_program():
    nc = bass.Bass("TRN2", target_bir_lowering=False)
    P = Prog()

    def dram_in(name, shape):
        return nc.dram_tensor(name, list(shape), F32, kind="ExternalInput").ap()

    xh = dram_in("xh", [17 * 128, D])
    wqk_d = dram_in("wqk", [D, 640])
    wvu_d = dram_in("wvu", [D, 640])
    wout_d = dram_in("wout", [D, D])
    wpool_d = dram_in("wpool", [4, 128, 128])
    wgate_d = dram_in("wgate", [D, DFF])
    wup_d = dram_in("wup", [D, DFF])
    wdown_d = dram_in("wdown", [DFF, D])
    gmix_d = dram_in("gmix", [128, 8])
    gffn_d = dram_in("gffn", [128, 8])
    bqk_d = dram_in("bqk", [128, 5])
    bqksw_d = dram_in("bqksw", [128, 5])
    bvu_d = dram_in("bvu", [1, 640])
    bout_d = dram_in("bout", [1, D])
    gfin_d = dram_in("gfin", [1, D])
    bpool_d = dram_in("bpool", [128, 4])
    pscale_d = dram_in("pscale", [128, 4])
    sinks_d = dram_in("sinks", [1, 8])
    flag_d = dram_in("flag", [128, 1])
    ident_d = dram_in("ident", [128, 128])
    apool_d = dram_in("apool", [16, 128, 128])
    cos_d = dram_in("cosT", [128, 17 * 128])
    sin_d = dram_in("sinT", [128, 17 * 128])
    out_d = nc.dram_tensor("out", [TOK, D], F32, kind="ExternalOutput").ap()

    total = nc.sbuf_bytes_remaining
    asize = (total - 2048) // 64 * 64
    arena = nc.alloc_sbuf_tensor("arena", [128, asize // 4], F32)
    base = nc.lookup_mloc(arena).addr
    assert base % 32 == 0
    sb = Sbuf(nc, base, base + asize)

    x1, _ = sb.alloc("x1", [128, NB, D], F32)
    h2T, h2T_off = sb.alloc("h2T", [128, 8, TOK], BF16)
    xhalo, _ = sb.alloc("xhalo", [128, D], F32, at=h2T_off)
    c_gmix, _ = sb.alloc("c_gmix", [128, 8], F32)
    c_gffn, _ = sb.alloc("c_gffn", [128, 8], F32)
    c_bqk, _ = sb.alloc("c_bqk", [128, 5], F32)
    c_bqksw, _ = sb.alloc("c_bqksw", [128, 5], F32)
    c_bpool, _ = sb.alloc("c_bpool", [128, 4], F32)
    c_pscale, _ = sb.alloc("c_pscale", [128, 4], F32)
    c_esink, _ = sb.alloc("c_esink", [128, 8], F32)
    c_flag, _ = sb.alloc("c_flag", [128, 1], F32)
    c_nhalf, _ = sb.alloc("c_nhalf", [128, 1], F32)
    c_ident, _ = sb.alloc("c_ident", [128, 128], BF16)
    c_bvu, _ = sb.alloc("c_bvu", [128, 640], F32)
    c_bout, _ = sb.alloc("c_bout", [128, D], F32)
    c_gfin, gfin_off = sb.alloc("c_gfin", [128, D], F32)
    c_eps, _ = sb.alloc("c_eps", [128, 1], F32)
    c_apool, _ = sb.alloc("c_apool", [128, 16, 128], BF16)
    ss, _ = sb.alloc("ss", [128, 64], F32)
    rstd, _ = sb.alloc("rstd", [128, 64], F32)
    ffn_bufs = []
    g0, _ = sb.alloc("gate0", [128, 8, 256], BF16)
    u0, _ = sb.alloc("up0", [128, 8, 256], BF16)
    d0, _ = sb.alloc("down0", [128, 2, D], BF16)
    ffn_bufs.append((g0, u0, d0))
    w_region = sb.off
    wqk, _ = sb.alloc("wqk", [128, 8, 640], BF16)
    wvu, _ = sb.alloc("wvu", [128, 8, 640], BF16)
    wout, _ = sb.alloc("wout", [128, 8, D], BF16)
    wpool, _ = sb.alloc("wpool", [128, 4, 128], BF16)
    o = w_region
    for i in (1, 2):
        g_, _ = sb.alloc(f"gate{i}", [128, 8, 256], BF16, at=o); o += 8 * 256 * 2
        u_, _ = sb.alloc(f"up{i}", [128, 8, 256], BF16, at=o); o += 8 * 256 * 2
        d_, _ = sb.alloc(f"down{i}", [128, 2, D], BF16, at=o); o += 2 * D * 2
        ffn_bufs.append((g_, u_, d_))
    assert o <= sb.off
    scratch = sb.off
    GT = GB * 128
    xs, _ = sb.alloc("xs", [128, D], BF16)
    junk, _ = sb.alloc("junk", [128, D], BF16, at=gfin_off)
    hT, _ = sb.alloc("hT", [128, 8, GT], BF16)
    xs2, _ = sb.alloc("xs2", [128, D], BF16)
    qT = [sb.alloc(f"qT{i}", [128, 4, GT], BF16)[0] for i in range(2)]
    K0, _ = sb.alloc("K0", [128, RING, 128], BF16)
    K1, _ = sb.alloc("K1", [128, RING, 128], BF16)
    vaug, _ = sb.alloc("vaug", [128, RING, 2, 128], BF16)
    utok, _ = sb.alloc("utok", [128, RING, 512], BF16)
    cosb = [sb.alloc(f"cos{i}", [128, GT], F32)[0] for i in range(1)]
    sinb = [sb.alloc(f"sin{i}", [128, GT], F32)[0] for i in range(1)]
    ropeA, _ = sb.alloc("ropeA", [128, GT], F32)
    ropeB, _ = sb.alloc("ropeB", [128, GT], F32)
    mixedT, _ = sb.alloc("mixedT", [128, 4, GT], BF16)
    poolT, _ = sb.alloc("poolT", [128, 4, GT], BF16)
    attnT, _ = sb.alloc("attnT", [128, 4, GT], BF16)
    Pt = [sb.alloc(f"P{i}", [128, 4, 128], BF16)[0] for i in range(4)]
    rden = [sb.alloc(f"rden{i}", [128, 4, 128], F32)[0] for i in range(2)]
    mk, _ = sb.alloc("mk", [128, 2, 128], BF16)
    es_bf, _ = sb.alloc("es_bf", [128, 8], BF16)
    sel, _ = sb.alloc("sel", [128, 128], BF16)
    hT_alt, _ = sb.alloc("hT_alt", [128, 8, GT], BF16)
    phase1_end = sb.off
    sb2 = Sbuf(nc, scratch, sb.limit)
    sb2.n = 1000
    xs_2, _ = sb2.alloc("xs_2", [128, D], BF16)
    junk_2, _ = sb2.alloc("junk_2", [128, D], BF16)
    junk2 = junk_2
    sg0_t, _ = sb2.alloc("sg0", [128, 512], F32)
    sb2.alloc("xs2_hole", [128, D], BF16)
    actT = [sb2.alloc(f"actT{i}", [128, 512], BF16)[0] for i in range(4)]
    tmpf = [sb2.alloc(f"tmpf{i}", [128, D], F32)[0] for i in range(2)]
    sg1_t, _ = sb2.alloc("sg1", [128, 512], F32)
    sg = [sg0_t, sg1_t]
    print("SBUF plan: phase1 end", phase1_end, "phase2 end", sb2.off, "limit", sb.limit)

    banks = []
    for i in range(8):
        banks.append(nc.alloc_psum_tensor(f"bank{i}", [128, 512], F32))
    bank_bufs = [Buf(f"bank{i}") for i in range(8)]

    sems = {}
    dma_sems = {}

    def dma(q, out, in_, reads=(), writes=(), extra=()):
        return P.op(q, I("dma_start", out=out, in_=in_), reads, writes, dma=True, extra=extra)

    B = {}

    def buf(name):
        if name not in B:
            B[name] = Buf(name)
        return B[name]

    xbuf = [buf(f"x1_{i}") for i in range(NB)]
    xhalo_b = buf("xhalo")

    def xslot(b):
        return (xhalo[:, :], xhalo_b) if b == 0 else (x1[:, b - 1, :], xbuf[b - 1])

    def load_x(b, extra=()):
        ap, bb = xslot(b)
        dma("sp", ap, xh[b * 128:(b + 1) * 128, :], writes=[bb], extra=extra)

    P.op("pool", I("memset", c_eps[:], EPS), writes=[buf("c_eps")])
    P.op("pool", I("memset", c_nhalf[:], -0.5), writes=[buf("c_nhalf")])
    load_x(0)
    load_x(1)
    load_x(2)
    dma("pool", wqk[:], wqk_d.rearrange("(k p) n -> p k n", p=128), writes=[buf("wqk")])
    dma("pool", c_ident[:], ident_d[:, :], writes=[buf("ident")])
    dma("sp", c_gmix[:], gmix_d[:, :], writes=[buf("c_gmix")])
    dma("sp", c_bqk[:], bqk_d[:, :], writes=[buf("c_bqk")])
    dma("sp", c_bqksw[:], bqksw_d[:, :], writes=[buf("c_bqksw")])
    dma("sp", c_bvu[:], bvu_d.partition_broadcast(128).squeeze(1), writes=[buf("c_bvu")])
    dma("sp", c_flag[:], flag_d[:, :], writes=[buf("c_flag")])
    dma("pool", wvu[:], wvu_d.rearrange("(k p) n -> p k n", p=128), writes=[buf("wvu")])
    dma("pool", c_apool[:], apool_d.rearrange("m p t -> p m t"), writes=[buf("c_apool")])
    dma("pool", wpool[:], wpool_d.rearrange("g c d -> c g d"), writes=[buf("wpool")])
    dma("sp", c_esink[:], sinks_d.partition_broadcast(128).squeeze(1), writes=[buf("c_esink")])
    dma("sp", c_bpool[:], bpool_d[:, :], writes=[buf("c_bpool")])
    dma("sp", c_pscale[:], pscale_d[:, :], writes=[buf("c_pscale")])
    dma("sp", c_bout[:], bout_d.partition_broadcast(128).squeeze(1), writes=[buf("c_bout")])
    dma("sp", c_gffn[:], gffn_d[:, :], writes=[buf("c_gffn")])
    dma("pool", wout[:], wout_d.rearrange("(k p) n -> p k n", p=128), writes=[buf("wout")])
    for b in range(3, 5):
        load_x(b)

    ffn_b = [(buf(f"ffg{i}"), buf(f"ffu{i}"), buf(f"ffd{i}")) for i in range(3)]

    def load_ffn(j):
        r = j % 3
        g_, u_, d_ = ffn_bufs[r]
        gb_, ub_, db_ = ffn_b[r]
        extra_w = [buf("wqk"), buf("wvu"), buf("wout"), buf("wpool")] if r in (1, 2) and j < 3 else []
        dma("pool", g_[:], wgate_d[:, j * 256:(j + 1) * 256].rearrange("(k p) n -> p k n", p=128),
            writes=[gb_] + extra_w)
        dma("pool", u_[:], wup_d[:, j * 256:(j + 1) * 256].rearrange("(k p) n -> p k n", p=128),
            writes=[ub_] + extra_w)
        dma("pool", d_[:], wdown_d[j * 256:(j + 1) * 256, :].rearrange("(k p) n -> p k n", p=128),
            writes=[db_] + extra_w)

    P.op("pool", I("memset", K0[:], 0.0), writes=[buf("K0z")])
    P.op("pool", I("memset", K1[:], 0.0), writes=[buf("K1z")])

    ss_col = [0]
    BK_T0, BK_QV, BK_U, BK_M, BK_S0, BK_S1, BK_O0, BK_O1 = 0, 1, 2, 3, 4, 5, 6, 7
    NEG = -30000.0
    bankQ = bankV = bank_bufs[1]

    P.op("pool", I("memset", mk[:], 0.0), writes=[buf("maskb0"), buf("maskb1")])
    P.op("pool", I("affine_select", out=mk[:, 0, :], in_=mk[:, 0, :], pattern=[[-1, 128]],
                   compare_op=ALU.is_ge, fill=NEG, base=-1, channel_multiplier=1),
         reads=[buf("maskb0")], writes=[buf("maskb0")])
    P.op("pool", I("affine_select", out=mk[:, 1, :], in_=mk[:, 1, :], pattern=[[1, 128]],
                   compare_op=ALU.is_ge, fill=NEG, base=0, channel_multiplier=-1),
         reads=[buf("maskb1")], writes=[buf("maskb1")])
    P.op("pool", I("memset", sel[:, :], 0.0), writes=[buf("est_s0")])
    P.op("pool", I("memset", sel[0:1, 64:128], 1.0), reads=[buf("est_s0")], writes=[buf("est_s1")])
    est_bufs = [buf("est"), buf("est_s0"), buf("est_s1")]

    def stats(cols, x_list):
        c0 = cols[0]
        n = len(cols)
        sbufs = []
        for c, (x_ap, x_buf) in zip(cols, x_list):
            sb_ = buf(f"ss{c}")
            sbufs.append(sb_)
            P.op("act", I("activation", out=junk[:], in_=x_ap, func=AF.Square, accum_out=ss[:, c:c + 1]),
                 reads=[x_buf], writes=[buf("junk"), sb_])
        rbs = [buf(f"rstd{c}") for c in cols]
        P.op("act", I("activation", out=rstd[:, c0:c0 + n], in_=ss[:, c0:c0 + n], func=AF.Ln,
                      scale=1.0 / D, bias=c_eps[:, 0:1]),
             reads=sbufs + [buf("c_eps")], writes=rbs)
        P.op("act", I("activation", out=rstd[:, c0:c0 + n], in_=rstd[:, c0:c0 + n], func=AF.Exp, scale=-0.5),
             reads=rbs, writes=rbs)

    def norm_T(x_ap, x_buf, c, gcol, dstT, dst_bufs, col0, tagT, xs_t, xs_name, tbank):
        rb_ = buf(f"rstd{c}")
        P.op("act", I("activation", out=xs_t[:], in_=x_ap, func=AF.Copy, scale=rstd[:, c:c + 1]),
             reads=[x_buf, rb_], writes=[buf(xs_name)])
        yield
        yield
        tb = banks[tbank]
        tbf = tb[:, :].bitcast(BF16)
        for kc in range(8):
            P.op("pe", I("transpose", out=tbf[:, kc * 128:(kc + 1) * 128],
                         in_=xs_t[:, kc * 128:(kc + 1) * 128], identity=c_ident[:]),
                 reads=[buf(xs_name), buf("ident")], writes=[bank_bufs[tbank]])
        P.op("dve", I("tensor_tensor",
                      out=dstT[:, 0:8, col0:col0 + 128],
                      in0=tbf.rearrange("p (k t) -> p k t", k=8),
                      in1=gcol[:, 0:8].unsqueeze(2).to_broadcast([128, 8, 128]), op=ALU.mult),
             reads=[bank_bufs[tbank], buf(tagT)], writes=dst_bufs)
        yield

    hT_b = buf("hT")
    hT_list = [hT, hT_alt]
    hT_bufs = [hT_b, buf("hT_alt")]
    qT_bs = [buf("qT0"), buf("qT1")]
    Kb = [buf(f"K_{i}") for i in range(RING)]
    Vb = [buf(f"V_{i}") for i in range(RING)]
    Ub = [buf(f"U_{i}") for i in range(RING)]
    tabb = [buf("tab0"), buf("tab1")]
    h2T_b = [buf(f"h2T_{i}") for i in range(NB)]

    groups = [[0]] + [list(range(1 + GB * i, 1 + GB * (i + 1))) for i in range(NB // GB)]

    ss_col[0] = 17

    def stage1(gi):
        hT = hT_list[gi % 2]
        hT_b = hT_bufs[gi % 2]
        blocks = groups[gi]
        nb = len(blocks)
        ntok = nb * 128
        b0 = blocks[0]
        halo = (b0 == 0)
        qTg = qT[gi % 2]
        qT_b = qT_bs[gi % 2]
        tb_i = 0
        t_op = dma("sp", cosb[tb_i][:, 0:ntok], cos_d[:, b0 * 128:b0 * 128 + ntok], writes=[tabb[tb_i]])
        if gi == 1:
            for b_ in range(5, 17):
                load_x(b_, extra=[t_op])
            stats(list(range(5, 11)), [xslot(b_) for b_ in range(5, 11)])
            stats(list(range(11, 17)), [xslot(b_) for b_ in range(11, 17)])
        if gi == 2:
            load_ffn(0)
        dma("sp", sinb[tb_i][:, 0:ntok], sin_d[:, b0 * 128:b0 * 128 + ntok], writes=[tabb[tb_i]])
        for li, b in enumerate(blocks):
            xap, xb_ = xslot(b)
            yield from norm_T(xap, xb_, b, c_gmix, hT, [hT_b], li * 128, "c_gmix", xs, "xs", BK_T0)
            if not halo:
                P.op("pool", I("tensor_tensor", out=xap, in0=xap, in1=c_bout[:], op=ALU.add),
                     reads=[xb_, buf("c_bout")], writes=[xb_])
        for li, b in enumerate(blocks):
            s = b % RING
            for kc in range(8):
                P.op("pe", I("matmul", out=banks[BK_U][:, :], lhsT=hT[:, kc, li * 128:(li + 1) * 128],
                             rhs=wvu[:, kc, 0:512], start=(kc == 0), stop=(kc == 7)),
                     reads=[buf("wvu"), hT_b], writes=[bank_bufs[BK_U]])
            for kc in range(8):
                P.op("pe", I("matmul", out=banks[BK_QV][:, 256:384], lhsT=hT[:, kc, li * 128:(li + 1) * 128],
                             rhs=wvu[:, kc, 512:640], start=(kc == 0), stop=(kc == 7)),
                     reads=[buf("wvu"), hT_b], writes=[bankV])
            yield
            yield
            P.op("dve", I("tensor_tensor", out=utok[:, s, :], in0=banks[BK_U][:, :], in1=c_bvu[:, 0:512],
                          op=ALU.add),
                 reads=[bank_bufs[BK_U], buf("c_bvu")], writes=[Ub[s]])
            P.op("dve", I("tensor_tensor",
                          out=vaug[:, s, :, 0:64],
                          in0=banks[BK_QV][:, 256:384].rearrange("p (k d) -> p k d", k=2),
                          in1=c_bvu[:, 512:640].rearrange("p (k d) -> p k d", k=2), op=ALU.add),
                 reads=[bankV, buf("c_bvu")], writes=[Vb[s]])
            P.op("pool", I("memset", vaug[:, s, :, 64:128], 1.0), reads=[], writes=[buf(f"Vones_{s}")])
            if b == 0:
                P.op("dve", I("tensor_scalar", out=vaug[:, s, :, :], in0=vaug[:, s, :, :],
                              scalar1=c_flag[:, 0:1], scalar2=None, op0=ALU.mult),
                     reads=[Vb[s], buf(f"Vones_{s}"), buf("c_flag")], writes=[Vb[s], buf(f"Vones_{s}")])
            yield
        chunks = [4] if halo else CHUNK_ORDER
        for ci, c in enumerate(chunks):
            for kc in range(8):
                P.op("pe", I("matmul", out=banks[BK_QV][:, 0:ntok], lhsT=wqk[:, kc, c * 128:(c + 1) * 128],
                             rhs=hT[:, kc, 0:ntok], start=(kc == 0), stop=(kc == 7)),
                     reads=[buf("wqk"), hT_b], writes=[bankQ])
            yield
            yield
            Z = banks[BK_QV]
            P.op("dve", I("scalar_tensor_tensor",
                          out=ropeA[:, 0:ntok], in0=Z[:, 0:ntok], scalar=c_bqk[:, c:c + 1],
                          in1=cosb[tb_i][:, 0:ntok], op0=ALU.add, op1=ALU.mult),
                 reads=[bankQ, buf("c_bqk"), tabb[tb_i]], writes=[buf("ropeA")])
            P.op("dve", I("scalar_tensor_tensor",
                          out=ropeB[0:64, 0:ntok], in0=Z[64:128, 0:ntok], scalar=c_bqksw[0:64, c:c + 1],
                          in1=sinb[tb_i][0:64, 0:ntok], op0=ALU.add, op1=ALU.mult),
                 reads=[bankQ, buf("c_bqksw"), tabb[tb_i]], writes=[buf("ropeB0")])
            P.op("dve", I("scalar_tensor_tensor",
                          out=ropeB[64:128, 0:ntok], in0=Z[0:64, 0:ntok], scalar=c_bqksw[64:128, c:c + 1],
                          in1=sinb[tb_i][64:128, 0:ntok], op0=ALU.add, op1=ALU.mult),
                 reads=[bankQ, buf("c_bqksw"), tabb[tb_i]], writes=[buf("ropeB1")])
            rb = [buf("ropeA"), buf("ropeB0"), buf("ropeB1")]
            if c < 4:
                P.op("pool", I("tensor_tensor", out=qTg[:, c, 0:ntok], in0=ropeA[:, 0:ntok],
                               in1=ropeB[:, 0:ntok], op=ALU.add),
                     reads=rb, writes=[qT_b])
            else:
                for li, b in enumerate(blocks):
                    s = b % RING
                    for (q0, Kt) in ((0, K0), (32, K1), (64, K0), (96, K1)):
                        P.op("pool", I("tensor_tensor",
                                       out=Kt[q0:q0 + 32, s, :], in0=ropeA[q0:q0 + 32, li * 128:(li + 1) * 128],
                                       in1=ropeB[q0:q0 + 32, li * 128:(li + 1) * 128], op=ALU.add),
                             reads=rb + [buf("K0z"), buf("K1z")], writes=[Kb[s]])
            yield

    def stage23(gi):
        blocks = groups[gi]
        nb = len(blocks)
        ntok = nb * 128
        tb_i = gi % 2
        qTg = qT[tb_i]
        qT_b = qT_bs[tb_i]
        for li, b in enumerate(blocks):
            s, sp_ = b % RING, (b - 1) % RING
            first = 8 if b == 1 else 0
            Mb = banks[BK_M]
            for g in range(4):
                P.op("pe", I("matmul", out=Mb[:, g * 128:(g + 1) * 128], lhsT=utok[:, s, g * 128:(g + 1) * 128],
                             rhs=c_apool[:, first + g, :], start=True, stop=False),
                     reads=[Ub[s], buf("c_apool")], writes=[bank_bufs[BK_M]])
                P.op("pe", I("matmul", out=Mb[:, g * 128:(g + 1) * 128], lhsT=utok[:, sp_, g * 128:(g + 1) * 128],
                             rhs=c_apool[:, first + 4 + g, :], start=False, stop=True),
                     reads=[Ub[sp_], buf("c_apool")], writes=[bank_bufs[BK_M]])
            yield
            P.op("act", I("activation",
                          out=mixedT[:, 0:4, li * 128:(li + 1) * 128],
                          in_=Mb[:, :].rearrange("p (g t) -> p g t", g=4), func=AF.Copy),
                 reads=[bank_bufs[BK_M]], writes=[buf("mixedT")])
            yield
        for g in range(4):
            bk = BK_M
            P.op("pe", I("matmul", out=banks[bk][:, 0:ntok], lhsT=wpool[:, g, :],
                         rhs=mixedT[:, g, 0:ntok], start=True, stop=True),
                 reads=[buf("wpool"), buf("mixedT")], writes=[bank_bufs[bk]])
            yield
            P.op("dve", I("tensor_scalar",
                          out=poolT[:, g, 0:ntok], in0=banks[bk][:, 0:ntok], scalar1=c_bpool[:, g:g + 1],
                          scalar2=c_pscale[:, g:g + 1], op0=ALU.add, op1=ALU.mult),
                 reads=[bank_bufs[bk], buf("c_bpool"), buf("c_pscale")], writes=[buf("poolT")])
        for li, b in enumerate(blocks):
            tiles = [(kv, ki) for kv in range(2) for ki in range(2)]

            def score(i):
                kv, ki = tiles[i]
                Kt = K0 if kv == 0 else K1
                s = (b - 1 + ki) % RING
                Sb = BK_S0 + (i % 2)
                S3 = banks[Sb][:, :].rearrange("p (c q) -> p c q", c=4)
                P.op("pe", I("matmul", out=S3, lhsT=Kt[:, s, :], rhs=qTg[:, 0:4, li * 128:(li + 1) * 128],
                             start=True, stop=False),
                     reads=[Kb[s], qT_b], writes=[bank_bufs[Sb]])
                P.op("pe", I("matmul", out=S3, lhsT=c_ident[:],
                             rhs=mk[:, ki, :].unsqueeze(1).to_broadcast([128, 4, 128]),
                             start=False, stop=True),
                     reads=[buf("ident"), buf(f"maskb{ki}")], writes=[bank_bufs[Sb]])

            def expo(i):
                Sb = BK_S0 + (i % 2)
                P.op("act", I("activation", out=Pt[i][:, :, :],
                              in_=banks[Sb][:, :].rearrange("p (c q) -> p c q", c=4), func=AF.Exp, scale=0.125),
                     reads=[bank_bufs[Sb]], writes=[buf(f"P{i}")])

            def pv(i):
                kv, ki = tiles[i]
                s = (b - 1 + ki) % RING
                Ob = BK_O0 + kv
                O3 = banks[Ob][:, :].rearrange("p (c q) -> p c q", c=4)
                P.op("pe", I("matmul", out=O3, lhsT=vaug[:, s, kv, :], rhs=Pt[i][:, :, :],
                             start=(ki == 0), stop=False),
                     reads=[Vb[s], buf(f"Vones_{s}"), buf(f"P{i}")], writes=[bank_bufs[Ob]])
                if ki == 1:
                    P.op("pe", I("matmul", out=O3, lhsT=sel[:, :],
                                 rhs=es_bf[:, kv * 4:(kv + 1) * 4].unsqueeze(2).to_broadcast([128, 4, 128]),
                                 start=False, stop=True),
                         reads=est_bufs, writes=[bank_bufs[Ob]])

            def normalise(kv):
                Ob = BK_O0 + kv
                O3 = banks[Ob][:, :].rearrange("p (c q) -> p c q", c=4)
                rd = rden[kv]
                rdb = buf(f"rden{kv}")
                P.op("act", I("activation", out=rd[64:128, :, :], in_=O3[64:128, :, :], func=AF.Ln),
                     reads=[bank_bufs[Ob]], writes=[rdb])
                P.op("act", I("activation", out=rd[64:128, :, :], in_=rd[64:128, :, :], func=AF.Exp,
                              scale=-1.0),
                     reads=[rdb], writes=[rdb])

            def finish(kv):
                Ob = BK_O0 + kv
                O3 = banks[Ob][:, :].rearrange("p (c q) -> p c q", c=4)
                rd = rden[kv]
                rdb = buf(f"rden{kv}")
                for j in range(2):
                    P.op("dve", I("tensor_tensor",
                                  out=attnT[64 * j:64 * j + 64, 2 * kv:2 * kv + 2, li * 128:(li + 1) * 128],
                                  in0=O3[0:64, j::2, :], in1=rd[64:128, j::2, :], op=ALU.mult),
                         reads=[bank_bufs[Ob], rdb], writes=[buf("attnT")])

            score(0)
            score(1)
            yield
            expo(0)
            expo(1)
            yield
            pv(0)
            score(2)
            yield
            pv(1)
            score(3)
            expo(2)
            yield
            normalise(0)
            expo(3)
            yield
            pv(2)
            pv(3)
            finish(0)
            yield
            normalise(1)
            yield
            finish(1)
            yield
        for li, b in enumerate(blocks):
            xap, xb_ = xslot(b)
            for half in range(2):
                bk = BK_S0 + half
                for kc in range(8):
                    src = attnT if kc < 4 else poolT
                    P.op("pe", I("matmul",
                                 out=banks[bk][:, :], lhsT=src[:, kc % 4, li * 128:(li + 1) * 128],
                                 rhs=wout[:, kc, half * 512:(half + 1) * 512], start=(kc == 0), stop=(kc == 7)),
                         reads=[buf("attnT"), buf("poolT"), buf("wout")], writes=[bank_bufs[bk]])
                yield
            for half in range(2):
                bk = BK_S0 + half
                P.op("dve", I("tensor_tensor",
                              out=x1[:, b - 1, half * 512:(half + 1) * 512], in0=banks[bk][:, :],
                              in1=x1[:, b - 1, half * 512:(half + 1) * 512], op=ALU.add),
                     reads=[bank_bufs[bk], xb_], writes=[xb_])
            yield

    def tail(gi):
        blocks = groups[gi]
        for b in blocks:
            xap, xb_ = xslot(b)
            c = ss_col[0]
            ss_col[0] += 1
            stats([c], [(xap, xb_)])
            yield
            yield from norm_T(xap, xb_, c, c_gffn, h2T, [h2T_b[b - 1], xhalo_b], (b - 1) * 128, "c_gffn",
                              xs2, "xs2", BK_T0)

    def drain(gen):
        for _ in gen:
            pass

    def interleave(gens):
        gens = [g for g in gens if g is not None]
        alive = list(gens)
        while alive:
            for g in list(alive):
                try:
                    next(g)
                except StopIteration:
                    alive.remove(g)

    stats([0, 1, 2], [xslot(b) for b in (0, 1, 2)])
    drain(stage1(0))
    P.op("act", I("activation", out=c_esink[:], in_=c_esink[:], func=AF.Exp),
         reads=[], writes=[buf("c_esink")])
    P.op("dve", I("tensor_copy", out=es_bf[:, :], in_=c_esink[:, :]),
         reads=[buf("c_esink")], writes=[buf("est")])
    stats([3, 4], [xslot(b) for b in (3, 4)])
    drain(stage1(1))
    ng = len(groups)
    for gi in range(1, ng):
        interleave([stage23(gi),
                    stage1(gi + 1) if gi + 1 < ng else None,
                    tail(gi - TAIL_LAG) if gi - TAIL_LAG >= 1 else None])
    for g_ in range(max(1, ng - TAIL_LAG), ng):
        drain(tail(g_))

    final_ops = []
    if DEBUG_X1:
        if DEBUG_X1 == 2:
            allb = list(B.values())
            for t in range(NB):
                P.op("dve", I("tensor_copy", out=x1[:, t, :], in_=h2T[:, t // 2, (t % 2) * 1024:(t % 2 + 1) * 1024]),
                     reads=allb, writes=[xbuf[t]])
        for t in range(NB):
            o = dma("sp", out_d[t * 128:(t + 1) * 128, :], x1[:, t, :], reads=[xbuf[t]])
            final_ops.append(o)
    else:
        dma("sp", c_gfin[:], gfin_d.partition_broadcast(128).squeeze(1), writes=[buf("c_gfin"), buf("junk")])

        def bw(i):
            return [bank_bufs[i]]

        SCHED_LIMIT[0] = P.nseq
        P.op("pool", I("memset", c_nhalf[:], -0.5),
             reads=[], writes=[buf("c_nhalf"), hT_b, buf("hT_alt"), qT_bs[0], qT_bs[1], buf("p2ok")]
             + Kb + Vb + Ub + [buf(f"Vones_{i}") for i in range(RING)] + [buf("K0z"), buf("K1z")])

        for j in range(NPASS):
            r = j % 3
            if j == 0:
                load_ffn(1)
                load_ffn(2)
            elif j + 2 < NPASS:
                load_ffn(j + 2)
            g_, u_, d_ = ffn_bufs[r]
            gb_, ub_, db_ = ffn_b[r]
            for G in range(4):
                hb = [h2T_b[G * 4 + t] for t in range(4)]
                for fc in range(2):
                    ai = (G * 2 + fc) % 4
                    si = fc
                    gbk, ubk = (0, 1) if fc == 0 else (2, 3)
                    if P0_BANKS and j == 0:
                        gbk, ubk = (1, 2) if fc == 0 else (3, 4)
                    for kc in range(8):
                        P.op("pe", I("matmul",
                            out=banks[gbk][:, :], lhsT=g_[:, kc, fc * 128:(fc + 1) * 128],
                            rhs=h2T[:, kc, G * 512:(G + 1) * 512], start=(kc == 0), stop=(kc == 7)),
                            reads=[gb_] + hb, writes=bw(gbk))
                    for kc in range(8):
                        P.op("pe", I("matmul",
                            out=banks[ubk][:, :], lhsT=u_[:, kc, fc * 128:(fc + 1) * 128],
                            rhs=h2T[:, kc, G * 512:(G + 1) * 512], start=(kc == 0), stop=(kc == 7)),
                            reads=[ub_] + hb, writes=bw(ubk))
                    P.op("act", I("activation", out=sg[si][:], in_=banks[gbk][:, :],
                                                                       func=AF.Silu),
                         reads=[bank_bufs[gbk], buf("p2ok")], writes=[buf(f"sg{si}")])
                    P.op("dve", I("tensor_tensor",
                        out=actT[ai][:], in0=banks[ubk][:, :], in1=sg[si][:], op=ALU.mult),
                        reads=[bank_bufs[ubk], buf(f"sg{si}"), buf("p2ok")], writes=[buf(f"actT{ai}")])
                for t in range(4):
                    blk = G * 4 + t
                    for half in range(2):
                        dbk = 4 + ((t * 2 + half) % 4)
                        if P0_BANKS and j == 0:
                            dbk = 5 + ((t * 2 + half) % 3)
                        for fc in range(2):
                            ai = (G * 2 + fc) % 4
                            P.op("pe", I("matmul",
                                out=banks[dbk][:, :], lhsT=actT[ai][:, t * 128:(t + 1) * 128],
                                rhs=d_[:, fc, half * 512:(half + 1) * 512], start=(fc == 0), stop=(fc == 1)),
                                reads=[db_, buf(f"actT{ai}")], writes=[bank_bufs[dbk]])
                        P.op("dve", I("tensor_tensor",
                            out=x1[:, blk, half * 512:(half + 1) * 512], in0=banks[dbk][:, :],
                            in1=x1[:, blk, half * 512:(half + 1) * 512], op=ALU.add),
                            reads=[bank_bufs[dbk], xbuf[blk]], writes=[xbuf[blk]])
                    if j == NPASS - 1:
                        c = ss_col[0] % 64
                        ss_col[0] += 1
                        sb_ = buf(f"ss{c}")
                        rb_ = buf(f"rstd{c}")
                        xap = x1[:, blk, :]
                        P.op("act", I("activation",
                            out=junk2[:], in_=xap, func=AF.Square, accum_out=ss[:, c:c + 1]),
                            reads=[xbuf[blk], buf("p2ok")], writes=[buf("junk2"), sb_])
                        P.op("act", I("activation", out=rstd[:, c:c + 1], in_=ss[:, c:c + 1], func=AF.Identity,
                                      scale=1.0 / D, bias=c_eps[:, 0:1]),
                             reads=[sb_, buf("c_eps")], writes=[rb_])
                        P.op("pool", I("tensor_tensor",
                            out=rstd[:, c:c + 1], in0=rstd[:, c:c + 1], in1=c_nhalf[:], op=ALU.pow),
                            reads=[rb_, buf("c_nhalf")], writes=[rb_])
                        if blk >= NB - 4:
                            P.op("dve", I("scalar_tensor_tensor", out=xap, in0=xap, scalar=rstd[:, c:c + 1],
                                          in1=c_gfin[:], op0=ALU.mult, op1=ALU.mult),
                                 reads=[xbuf[blk], rb_, buf("c_gfin")], writes=[xbuf[blk]])
                        else:
                            tf = tmpf[blk % 2]
                            tfb = buf(f"tmpf{blk % 2}")
                            P.op("act", I("activation", out=tf[:], in_=xap, func=AF.Copy, scale=rstd[:, c:c + 1]),
                                 reads=[xbuf[blk], rb_, buf("p2ok")], writes=[tfb])
                            P.op("pool", I("tensor_tensor", out=xap, in0=tf[:], in1=c_gfin[:], op=ALU.mult),
                                 reads=[tfb, buf("c_gfin")], writes=[xbuf[blk]])
                        o = dma("sp", out_d[blk * 128:(blk + 1) * 128, :], xap, reads=[xbuf[blk]])
                        final_ops.append(o)

    import contextlib
    with contextlib.ExitStack() as es:
        for e in ("pe", "act", "dve", "pool"):
            sems[e] = es.enter_context(nc.semaphore(f"s_{e}"))
        for q, n in (("sp", 28), ("pool", 20)):
            dma_sems[q] = [es.enter_context(nc.semaphore(f"d_{q}{i}")) for i in range(n)]
        block = es.enter_context(nc.Block())
        if USE_SCHED:
            P.schedule(limit=None)
            print("scheduler: simulated end %.1f us" % (P.sim_end / 1e3))
        P.emit(nc, block, sems, dma_sems, final_ops)
    return nc


def _prep(inputs):
    f32 = np.float32
    x = np.asarray(inputs["x"], f32)
    w_in = np.asarray(inputs["w_in"], f32)[0]
    b_in = np.asarray(inputs["b_in"], f32)[0]
    cols = []
    for c in range(4):
        A, Bh = c, c + 4
        cols += list(range(A * 64, A * 64 + 32)) + list(range(Bh * 64, Bh * 64 + 32))
        cols += list(range(A * 64 + 32, A * 64 + 64)) + list(range(Bh * 64 + 32, Bh * 64 + 64))
    cols += list(range(512, 544)) + list(range(576, 608)) + list(range(544, 576)) + list(range(608, 640))
    cols = np.array(cols)
    wqk = np.ascontiguousarray(w_in[:, cols])
    bqk = np.ascontiguousarray(b_in[cols].reshape(5, 128).T)
    bqksw = np.ascontiguousarray(np.roll(b_in[cols].reshape(5, 128), 64, axis=1).T)
    vu_cols = np.array(list(range(768, 1280)) + list(range(640, 768)))
    wvu = np.ascontiguousarray(w_in[:, vu_cols])
    bvu = np.ascontiguousarray(b_in[vu_cols].reshape(1, 640))
    common = {
        "wqk": wqk, "wvu": wvu,
        "wout": np.ascontiguousarray(np.asarray(inputs["w_out"], f32)[0]),
        "wpool": np.ascontiguousarray(np.asarray(inputs["w_pool"], f32)[0]),
        "wgate": np.ascontiguousarray(np.asarray(inputs["w_gate"], f32)[0]),
        "wup": np.ascontiguousarray(np.asarray(inputs["w_up"], f32)[0]),
        "wdown": np.ascontiguousarray(np.asarray(inputs["w_down"], f32)[0]),
        "gmix": np.ascontiguousarray(np.asarray(inputs["g_mix"], f32)[0].reshape(8, 128).T),
        "gffn": np.ascontiguousarray(np.asarray(inputs["g_ffn"], f32)[0].reshape(8, 128).T),
        "bqk": bqk, "bqksw": bqksw, "bvu": bvu,
        "bout": np.ascontiguousarray(np.asarray(inputs["b_out"], f32)[0].reshape(1, D)),
        "gfin": np.ascontiguousarray(np.asarray(inputs["g_final"], f32).reshape(1, D)),
        "bpool": np.ascontiguousarray(np.asarray(inputs["b_pool"], f32)[0].T),
        "pscale": np.ascontiguousarray(np.asarray(inputs["pool_scale"], f32)[0].T),
        "sinks": np.ascontiguousarray(np.asarray(inputs["sinks"], f32)[0].reshape(1, 8)),
        "ident": np.eye(128, dtype=f32),
    }
    sizes = (2, 4, 8, 16)
    tp = np.arange(128)[:, None]
    t = np.arange(128)[None, :]
    apool = np.zeros((16, 128, 128), f32)
    for g, s in enumerate(sizes):
        delta = t - tp
        main = ((delta >= 0) & (delta < s)).astype(f32) / s - (delta == 0).astype(f32)
        dprev = t + 128 - tp
        prev = ((dprev >= 0) & (dprev < s)).astype(f32) / s
        cnt = np.minimum(t + 1, s).astype(f32)
        mainf = ((delta >= 0) & (delta < s)).astype(f32) / cnt - (delta == 0).astype(f32)
        apool[g] = main
        apool[4 + g] = prev
        apool[8 + g] = mainf
        apool[12 + g] = 0.0
    inv_freq = (1.0 / (10000.0 ** (np.arange(0, 64, 2, dtype=f32) / f32(64)))).astype(f32)
    maps = []
    for core in range(8):
        bidx, chunk = core // 4, core % 4
        t0 = chunk * TOK
        xhh = np.zeros((17 * 128, D), f32)
        if chunk > 0:
            xhh[0:128] = x[bidx, t0 - 128:t0]
        xhh[128:] = x[bidx, t0:t0 + TOK]
        pos = (np.arange(17 * 128) + t0 - 128).astype(f32)
        ang = (pos[:, None] * inv_freq[None, :]).astype(f32)
        cosv = np.cos(ang).astype(f32).T
        sinv = np.sin(ang).astype(f32).T
        cosT = np.ascontiguousarray(np.tile(cosv, (4, 1)))
        sinT = np.ascontiguousarray(np.concatenate([-sinv, -sinv, sinv, sinv], axis=0))
        ap = apool.copy()
        if chunk > 0:
            ap[8:12] = ap[0:4]
            ap[12:16] = ap[4:8]
        m = dict(common)
        m.update({
            "xh": xhh, "cosT": cosT, "sinT": sinT, "apool": ap,
            "flag": np.full((128, 1), 0.0 if chunk == 0 else 1.0, f32),
        })
        maps.append(m)
    return maps


_NC_CACHE = {}


def kernel(**inputs):
    maps = _prep(inputs)
    if "nc" not in _NC_CACHE:
        _NC_CACHE["nc"] = build_program()
    nc = _NC_CACHE["nc"]
    res = run_bass_kernel_spmd(nc, maps, core_ids=list(range(8)))
    outs = [np.asarray(r["out"], np.float32).reshape(TOK, D) for r in res.results]
    full = np.stack([np.concatenate(outs[0:4], axis=0), np.concatenate(outs[4:8], axis=0)], axis=0)
    return full.astype(np.float32)
```
